# Optimizing a Trainium2 kernel written in Bass

```python
import math
import jax, jax.numpy as jnp
from jax import lax
import numpy as np

D_MODEL = 1024
BATCH = 8
SEQ = 2048
DEPTH = 1
DEC_BATCH = 128
DEC_SEQ = 4
PAST_LEN = 16384
PAGE_SIZE = 128

D_MIX = D_MODEL
D_LRU = D_MIX // 2
D_SCONV = D_MIX - D_LRU
N_LRU_HEADS = 8
LRU_HEAD_DIM = D_LRU // N_LRU_HEADS
N_SCONV_GROUPS = 8
LRU_CONV_W = 4
SCONV_W = 3
RG_C = 8.0
D_FF = 3 * D_MODEL
N_MOD = 9
D_IN = 2 * D_LRU + 3 * D_SCONV
EPS = 1e-6

kernel_name = "hymba_rglru_shortconv_macaron_adaln_step"


def rmsnorm(x, g):
    xf = x.astype(jnp.float32)
    r = xf * lax.rsqrt(jnp.mean(xf * xf, axis=-1, keepdims=True) + EPS)
    return (r * g.astype(jnp.float32)).astype(x.dtype)


def modulate(h, shift, scale):
    return h * (1 + scale[:, None, :]) + shift[:, None, :]


def swiglu(h, w_gate, w_up, w_down):
    return (jax.nn.silu(h @ w_gate) * (h @ w_up)) @ w_down


def causal_dwconv(buf, u, w):
    K = w.shape[0]
    T = u.shape[1]
    full = jnp.concatenate([buf, u], axis=1)
    out = full[:, 0:T] * w[0]
    for k in range(1, K):
        out = out + full[:, k:k + T] * w[k]
    return out, full[:, -(K - 1):]


def rglru(xc, h0, w_a, b_a, w_x, b_x, lam):
    B, T, _ = xc.shape
    xh = xc.reshape(B, T, N_LRU_HEADS, LRU_HEAD_DIM)
    r = jax.nn.sigmoid(jnp.einsum('bthi,hij->bthj', xh, w_a) + b_a).reshape(B, T, D_LRU)
    i = jax.nn.sigmoid(jnp.einsum('bthi,hij->bthj', xh, w_x) + b_x).reshape(B, T, D_LRU)
    log_a = -RG_C * r.astype(jnp.float32) * jax.nn.softplus(-lam.astype(jnp.float32))
    a = jnp.exp(log_a)
    mult = jnp.sqrt(-jnp.expm1(2.0 * log_a))
    b = mult * (i.astype(jnp.float32) * xc.astype(jnp.float32))
    b = b.at[:, 0].add(a[:, 0] * h0.astype(jnp.float32))

    def combine(left, right):
        a1, b1 = left
        a2, b2 = right
        return a1 * a2, a2 * b1 + b2

    _, h = lax.associative_scan(combine, (a, b), axis=1)
    return h.astype(xc.dtype), h[:, -1].astype(xc.dtype)


def mixer(h, lru_buf, h0, sc_buf, w_mix_in, w_lru_conv, b_lru_conv, w_gate_a, b_gate_a,
          w_gate_x, b_gate_x, lru_lambda, w_sconv, w_mix_out):
    z = h @ w_mix_in
    x_l = z[..., 0:D_LRU]
    g_l = z[..., D_LRU:2 * D_LRU]
    o = 2 * D_LRU
    b_s = z[..., o:o + D_SCONV]
    c_s = z[..., o + D_SCONV:o + 2 * D_SCONV]
    v_s = z[..., o + 2 * D_SCONV:o + 3 * D_SCONV]
    xc, new_lru_buf = causal_dwconv(lru_buf, x_l, w_lru_conv)
    xc = xc + b_lru_conv
    hl, h_last = rglru(xc, h0, w_gate_a, b_gate_a, w_gate_x, b_gate_x, lru_lambda)
    lru_out = hl * jax.nn.gelu(g_l)
    u = c_s * v_s
    cv, new_sc_buf = causal_dwconv(sc_buf, u, w_sconv)
    sc_out = b_s * cv
    out = jnp.concatenate([lru_out, sc_out], axis=-1) @ w_mix_out
    return out, new_lru_buf, h_last, new_sc_buf


def layer(x, c, lru_buf, h0, sc_buf, w_ada, b_ada, g_ffn1, w1_gate, w1_up, w1_down, g_mix,
          w_mix_in, w_lru_conv, b_lru_conv, w_gate_a, b_gate_a, w_gate_x, b_gate_x, lru_lambda,
          w_sconv, w_mix_out, g_ffn2, w2_gate, w2_up, w2_down):
    B = x.shape[0]
    mod = (jax.nn.silu(c) @ w_ada + b_ada).reshape(B, N_MOD, D_MODEL)
    hn = modulate(rmsnorm(x, g_ffn1), mod[:, 0], mod[:, 1])
    x = x + 0.5 * mod[:, 2][:, None, :] * swiglu(hn, w1_gate, w1_up, w1_down)
    hn = modulate(rmsnorm(x, g_mix), mod[:, 3], mod[:, 4])
    mo, new_lru_buf, h_last, new_sc_buf = mixer(hn, lru_buf, h0, sc_buf, w_mix_in, w_lru_conv,
                                                b_lru_conv, w_gate_a, b_gate_a, w_gate_x, b_gate_x,
                                                lru_lambda, w_sconv, w_mix_out)
    x = x + mod[:, 5][:, None, :] * mo
    hn = modulate(rmsnorm(x, g_ffn2), mod[:, 6], mod[:, 7])
    x = x + 0.5 * mod[:, 8][:, None, :] * swiglu(hn, w2_gate, w2_up, w2_down)
    return x, new_lru_buf, h_last, new_sc_buf


def setup_inputs(seed: int = 0) -> dict:
    key = jax.random.key(seed)
    ks = jax.random.split(key, 32)
    f32 = jnp.float32
    n = lambda k, shape, s: (jax.random.normal(k, shape, f32) * s)
    a_base = jax.random.uniform(ks[20], (DEPTH, D_LRU), f32, 0.9, 0.999) ** (1.0 / RG_C)
    lru_lambda = jnp.log(a_base / (1.0 - a_base))
    return {
        "x_prompt": n(ks[0], (BATCH, SEQ, D_MODEL), 1.0),
        "x_sample": n(ks[1], (DEC_BATCH, DEC_SEQ, D_MODEL), 1.0),
        "state_lru_conv": n(ks[2], (DEPTH, DEC_BATCH, LRU_CONV_W - 1, D_LRU), 0.5),
        "state_lru_h": n(ks[3], (DEPTH, DEC_BATCH, D_LRU), 0.5),
        "state_sconv": n(ks[4], (DEPTH, DEC_BATCH, SCONV_W - 1, D_SCONV), 0.5),
        "c_prompt": n(ks[5], (BATCH, D_MODEL), 1.0),
        "c_sample": n(ks[6], (DEC_BATCH, D_MODEL), 1.0),
        "w_ada": n(ks[7], (DEPTH, D_MODEL, N_MOD * D_MODEL), D_MODEL ** -0.5),
        "b_ada": n(ks[8], (DEPTH, N_MOD * D_MODEL), 0.02),
        "g_ffn1": 1.0 + n(ks[9], (DEPTH, D_MODEL), 0.02),
        "w1_gate": n(ks[10], (DEPTH, D_MODEL, D_FF), D_MODEL ** -0.5),
        "w1_up": n(ks[11], (DEPTH, D_MODEL, D_FF), D_MODEL ** -0.5),
        "w1_down": n(ks[12], (DEPTH, D_FF, D_MODEL), D_FF ** -0.5),
        "g_mix": 1.0 + n(ks[13], (DEPTH, D_MODEL), 0.02),
        "w_mix_in": n(ks[14], (DEPTH, D_MODEL, D_IN), D_MODEL ** -0.5),
        "w_lru_conv": n(ks[15], (DEPTH, LRU_CONV_W, D_LRU), LRU_CONV_W ** -0.5),
        "b_lru_conv": n(ks[16], (DEPTH, D_LRU), 0.02),
        "w_gate_a": n(ks[17], (DEPTH, N_LRU_HEADS, LRU_HEAD_DIM, LRU_HEAD_DIM), LRU_HEAD_DIM ** -0.5),
        "b_gate_a": n(ks[18], (DEPTH, N_LRU_HEADS, LRU_HEAD_DIM), 0.02),
        "w_gate_x": n(ks[19], (DEPTH, N_LRU_HEADS, LRU_HEAD_DIM, LRU_HEAD_DIM), LRU_HEAD_DIM ** -0.5),
        "b_gate_x": n(ks[21], (DEPTH, N_LRU_HEADS, LRU_HEAD_DIM), 0.02),
        "lru_lambda": lru_lambda,
        "w_sconv": n(ks[22], (DEPTH, SCONV_W, D_SCONV), SCONV_W ** -0.5),
        "w_mix_out": n(ks[23], (DEPTH, D_MIX, D_MODEL), D_MIX ** -0.5),
        "g_ffn2": 1.0 + n(ks[24], (DEPTH, D_MODEL), 0.02),
        "w2_gate": n(ks[25], (DEPTH, D_MODEL, D_FF), D_MODEL ** -0.5),
        "w2_up": n(ks[26], (DEPTH, D_MODEL, D_FF), D_MODEL ** -0.5),
        "w2_down": n(ks[27], (DEPTH, D_FF, D_MODEL), D_FF ** -0.5),
        "final_gain": 1.0 + n(ks[28], (D_MODEL,), 0.02),
    }


def reference(x_prompt, x_sample, state_lru_conv, state_lru_h, state_sconv, c_prompt, c_sample,
              w_ada, b_ada, g_ffn1, w1_gate, w1_up, w1_down, g_mix, w_mix_in, w_lru_conv, b_lru_conv,
              w_gate_a, b_gate_a, w_gate_x, b_gate_x, lru_lambda, w_sconv, w_mix_out,
              g_ffn2, w2_gate, w2_up, w2_down, final_gain):
    Bp = x_prompt.shape[0]
    dt = x_prompt.dtype
    xp = x_prompt
    xs = x_sample
    p_conv, p_h, p_sc = [], [], []
    s_conv, s_h, s_sc = [], [], []
    for l in range(DEPTH):
        w = (w_ada[l], b_ada[l], g_ffn1[l], w1_gate[l], w1_up[l], w1_down[l], g_mix[l], w_mix_in[l],
             w_lru_conv[l], b_lru_conv[l], w_gate_a[l], b_gate_a[l], w_gate_x[l], b_gate_x[l],
             lru_lambda[l], w_sconv[l], w_mix_out[l], g_ffn2[l], w2_gate[l], w2_up[l], w2_down[l])
        z_conv = jnp.zeros((Bp, LRU_CONV_W - 1, D_LRU), dt)
        z_h = jnp.zeros((Bp, D_LRU), dt)
        z_sc = jnp.zeros((Bp, SCONV_W - 1, D_SCONV), dt)
        xp, nc, nh, ns = layer(xp, c_prompt, z_conv, z_h, z_sc, *w)
        p_conv.append(nc); p_h.append(nh); p_sc.append(ns)
        xs, nc, nh, ns = layer(xs, c_sample, state_lru_conv[l], state_lru_h[l], state_sconv[l], *w)
        s_conv.append(nc); s_h.append(nh); s_sc.append(ns)
    y_prompt = rmsnorm(xp, final_gain)
    y_sample = rmsnorm(xs, final_gain)
    new_lru_conv_prompt = jnp.stack(p_conv)
    new_lru_h_prompt = jnp.stack(p_h)
    new_sconv_prompt = jnp.stack(p_sc)
    new_lru_conv_sample = jnp.stack(s_conv)
    new_lru_h_sample = jnp.stack(s_h)
    new_sconv_sample = jnp.stack(s_sc)
    return (y_prompt, y_sample, new_lru_conv_prompt, new_lru_h_prompt, new_sconv_prompt,
            new_lru_conv_sample, new_lru_h_sample, new_sconv_sample)
```

```python
import numpy as np
from contextlib import ExitStack
import concourse.bass as bass
import concourse.mybir as mybir
from concourse.bass_utils import run_bass_kernel_spmd

F32 = mybir.dt.float32
BF16 = mybir.dt.bfloat16
AF = mybir.ActivationFunctionType
ALU = mybir.AluOpType

NCORES = 8
D = 1024
DFF = 3072
NTOK = 2112
TW = [512, 512, 512, 512, 64]
TO = [0, 512, 1024, 1536, 2048]
NT = 5
EPS = 1e-6
NSLOT = 9
NTMP = 18
NBT = 4
TMPW = 516

P_G = [0, 8, 16]
P_GF = 24
P_BADA = 32
P_WLC = 104
P_BLC = 120
P_BGA = 124
P_BGX = 128
P_LAM = 132
P_WSC = 136
NPAR = 148
NSO = 408

ENGS = ("pe", "act", "dve", "pool", "sp")


class Prog:
    def __init__(self, nc, stack):
        self.nc = nc
        self.stack = stack
        self.streams = {e: [] for e in ENGS}
        self.sem = {e: stack.enter_context(nc.semaphore("s_" + e)) for e in ENGS}
        self.cnt = {e: 0 for e in ENGS}
        self.waited = {e: {} for e in ENGS}
        self.last_w = {}
        self.readers = {}
        self.dsem = {}
        self.dcnt = {}

    def _handle(self, k):
        return self.sem[k] if k in self.sem else self.dsem[k]

    def _deps(self, reads, writes):
        deps = {}

        def add(tok):
            if tok is not None and deps.get(tok[0], 0) < tok[1]:
                deps[tok[0]] = tok[1]
        for k in reads:
            add(self.last_w.get(k))
        for k in writes:
            add(self.last_w.get(k))
            for t in self.readers.get(k, ()):
                add(t)
        return deps

    def _emit_waits(self, eng, deps, skip=()):
        for k, v in deps.items():
            if k in skip or self.waited[eng].get(k, 0) >= v:
                continue
            self.waited[eng][k] = v
            h = self._handle(k)
            self.streams[eng].append(lambda E, h=h, v=v: E.wait_ge(h, v))

    def _record(self, tok, reads, writes):
        for k in writes:
            self.last_w[k] = tok
            self.readers[k] = []
        for k in reads:
            self.readers.setdefault(k, []).append(tok)

    def op(self, eng, fn, reads=(), writes=()):
        deps = self._deps(reads, writes)
        self._emit_waits(eng, deps)
        self.cnt[eng] += 1
        tok = (eng, self.cnt[eng])
        h = self.sem[eng]
        self.streams[eng].append(lambda E, fn=fn, h=h: fn(E).then_inc(h, 1))
        self._record(tok, reads, writes)
        return tok

    def mm(self, out, pairs, reads=(), writes=(), start=True, stop=True):
        deps = self._deps(reads, writes)
        self._emit_waits("pe", deps, skip=("pe",))
        self.cnt["pe"] += 1
        tok = ("pe", self.cnt["pe"])
        h = self.sem["pe"]
        n = len(pairs)
        for i, (l, r) in enumerate(pairs):
            st_ = bool(start and i == 0)
            sp_ = bool(stop and i == n - 1)
            if i == n - 1:
                self.streams["pe"].append(
                    lambda E, l=l, r=r, st_=st_, sp_=sp_: E.matmul(out, l, r, start=st_, stop=sp_).then_inc(h, 1))
            else:
                self.streams["pe"].append(
                    lambda E, l=l, r=r, st_=st_, sp_=sp_: E.matmul(out, l, r, start=st_, stop=sp_))
        self._record(tok, reads, writes)
        return tok

    def dma(self, q, semname, out, in_, reads=(), writes=(), skip=(), **kw):
        if semname not in self.dsem:
            self.dsem[semname] = self.stack.enter_context(self.nc.semaphore("d_" + semname))
            self.dcnt[semname] = 0
        deps = self._deps(reads, writes)
        self._emit_waits(q, deps, skip=skip)
        self.dcnt[semname] += 16
        tok = (semname, self.dcnt[semname])
        h = self.dsem[semname]
        self.streams[q].append(lambda E, h=h: E.dma_start(out=out, in_=in_, **kw).then_inc(h, 16))
        self._record(tok, reads, writes)
        return tok

    def final_wait(self, eng, keys):
        self._emit_waits(eng, self._deps(keys, ()))

    def run(self, eng, E):
        for f in self.streams[eng]:
            f(E)


class FreeList:
    def __init__(self, items):
        self.free = list(items)

    def get(self):
        return self.free.pop(0)

    def put(self, it):
        self.free.append(it)


class WRing:
    def __init__(self, P, slot_ap):
        self.P = P
        self.slot_ap = slot_ap
        self.busy = [False] * NSLOT
        self.nxt = 0
        self.pending = []

    def request(self, parts):
        h = {"parts": parts, "slot": None}
        self.pending.append(h)
        self.pump()
        return h

    def pump(self):
        while self.pending and not self.busy[self.nxt]:
            h = self.pending.pop(0)
            s = self.nxt
            h["slot"] = s
            self.busy[s] = True
            self.nxt = (s + 1) % NSLOT
            for dst_fn, src in h["parts"]:
                self.P.dma("pool", "W%d" % s, dst_fn(self.slot_ap(s)), src,
                           writes=[("W", s)], skip=("W%d" % s,), max_dma_last_dim=4096)

    def release(self, h):
        self.busy[h["slot"]] = False
        self.pump()


def build_program():
    nc = bass.Bass("TRN2", target_bir_lowering=False)

    def din(name, shape):
        return nc.dram_tensor(name, shape, F32, kind="ExternalInput").ap()

    xT = din("xT", [D, NTOK])
    cT = din("cT", [D, 17])
    par_d = din("par", [128, NPAR])
    stc_d = din("stc", [128, 4 * 16 * 3])
    sth_d = din("sth", [128, 4 * 16])
    sts_d = din("sts", [128, 4 * 16 * 2])
    wabd_d = din("wabd", [128, 4 * 128])
    wxbd_d = din("wxbd", [128, 4 * 128])
    w_ada = din("w_ada", [D, 9 * D])
    wff = [(din("w1g", [D, DFF]), din("w1u", [D, DFF]), din("w1d", [DFF, D])),
           (din("w2g", [D, DFF]), din("w2u", [D, DFF]), din("w2d", [DFF, D]))]
    wmi = din("wmi", [D, 2560])
    wmo = din("wmo", [D, D])
    yT = nc.dram_tensor("yT", [D, NTOK], F32, kind="ExternalOutput").ap()
    so_d = nc.dram_tensor("so", [128, NSO], F32, kind="ExternalOutput").ap()

    with ExitStack() as st:
        P = Prog(nc, st)

        def sb(name, shape, dt):
            return st.enter_context(nc.sbuf_tensor(name, shape, dt))

        x_sb = sb("x_sb", [128, 8, NTOK], F32)
        hn_sb = sb("hn_sb", [128, 8, NTOK], BF16)
        act_sb = sb("act_sb", [128, 4, NTOK], BF16)
        wring = sb("wring", [128, NSLOT, 2048], BF16)
        tmp_sb = sb("tmp_sb", [128, NTMP, TMPW], F32)
        btmp_sb = sb("btmp_sb", [128, NBT, 512], BF16)
        par = sb("par_sb", [128, NPAR], F32)
        der = sb("der_sb", [128, 16], F32)
        c_sb = sb("c_sb", [128, 8, 17], F32)
        scb = sb("scb", [128, 8, 17], BF16)
        mod = sb("mod_sb", [128, 72, 17], F32)
        gs = sb("gs_sb", [128, 3, 8, 17], F32)
        gm = sb("gm_sb", [128, 3, 8, 17], F32)
        stc = sb("stc_sb", [128, 4, 16, 3], F32)
        sth = sb("sth_sb", [128, 4, 16], F32)
        sts = sb("sts_sb", [128, 4, 16, 2], F32)
        so = sb("so_sb", [128, NSO], F32)
        ones_bf = sb("ones_bf", [128, 128], BF16)
        wabd = sb("wabd_sb", [128, 4, 128], BF16)
        wxbd = sb("wxbd_sb", [128, 4, 128], BF16)
        psb = [st.enter_context(nc.psum_tensor("ps%d" % i, [128, 512], F32)) for i in range(8)]
        block = st.enter_context(nc.Block())

        so_keys = []

        def so_copy(out, in_, reads):
            k_ = ("so", len(so_keys))
            so_keys.append(k_)
            P.op("pool", lambda E: E.tensor_copy(out=out, in_=in_), reads=reads, writes=[k_])

        tpool = FreeList(range(NTMP))
        bpool = FreeList(range(NBT))
        ppool = FreeList(range(8))

        def bank():
            return ppool.get()

        def bfree(*bs):
            for b_ in bs:
                ppool.put(b_)

        def tk(i):
            return ("tmp", i)

        def tt(i):
            return tmp_sb[:, i, :]

        def bk(i):
            return ("btmp", i)

        def bt(i):
            return btmp_sb[:, i, :]

        def xs(k, t):
            return x_sb[:, k, TO[t]:TO[t] + TW[t]]

        def hs(k, t):
            return hn_sb[:, k, TO[t]:TO[t] + TW[t]]

        def acs(c, t):
            return act_sb[:, c, TO[t]:TO[t] + TW[t]]

        def v3(ap, n=4):
            return ap.rearrange("p (s t) -> p s t", t=n)

        def bc(ap, n=4):
            return ap.unsqueeze(2).broadcast_to([128, 16, n])

        def pc(col):
            return par[:, col:col + 1]

        ring = WRing(P, lambda s: wring[:, s, :])

        def v_k256(a):
            return a.rearrange("p (k c) -> p k c", k=8)

        def v_2x1024(a):
            return a.rearrange("p (k c) -> p k c", k=2)

        def src_cols(w, c0, n):
            return w.rearrange("(k p) c -> p k c", p=128)[:, :, c0:c0 + n]

        def src_rows2(w, r0):
            return w[r0:r0 + 256, :].rearrange("(k p) c -> p k c", p=128)

        P.dma("sp", "ld_par", par[:, :], par_d[:, :], writes=["par"])
        P.dma("sp", "ld_c", c_sb[:, :, :], cT.rearrange("(k p) s -> p k s", p=128), writes=["c"])
        P.dma("sp", "ld_stc", stc[:, :, :, :], stc_d.rearrange("p (j s k) -> p j s k", j=4, s=16), writes=["stc"])
        P.dma("sp", "ld_sth", sth[:, :, :], sth_d.rearrange("p (j s) -> p j s", j=4), writes=["sth"])
        P.dma("sp", "ld_sts", sts[:, :, :, :], sts_d.rearrange("p (j s k) -> p j s k", j=4, s=16), writes=["sts"],
              skip=("ld_st",))
        P.dma("pool", "ld_wa", wabd[:, :, :], wabd_d.rearrange("p (j m) -> p j m", j=4), writes=["wabd"])
        P.dma("pool", "ld_wx", wxbd[:, :, :], wxbd_d.rearrange("p (j m) -> p j m", j=4), writes=["wxbd"],
              skip=("ld_bd",))
        for t in range(NT):
            for h2 in range(2):
                P.dma("sp", "ld_x%d_%d" % (t, h2), x_sb[:, 4 * h2:4 * h2 + 4, TO[t]:TO[t] + TW[t]],
                      xT.rearrange("(k p) n -> p k n", p=128)[:, 4 * h2:4 * h2 + 4, TO[t]:TO[t] + TW[t]],
                      writes=[("x", k, t) for k in range(4 * h2, 4 * h2 + 4)])
        P.op("dve", lambda E: E.memset(ones_bf[:, :], 1.0), writes=["ones"])

        P.op("dve", lambda E: E.tensor_scalar(out=der[:, 0:4], in0=par[:, P_LAM:P_LAM + 4], scalar1=-1.0, scalar2=None, op0=ALU.mult),
             reads=["par"], writes=["der"])
        P.op("dve", lambda E: E.scalar_tensor_tensor(out=der[:, 4:8], in0=der[:, 0:4], scalar=-1.0, in1=der[:, 0:4],
                                                     op0=ALU.mult, op1=ALU.max), reads=["der"], writes=["der"])
        P.op("act", lambda E: E.activation(out=der[:, 4:8], in_=der[:, 4:8], func=AF.Exp, scale=-1.0),
             reads=["der"], writes=["der"])
        P.op("act", lambda E: E.activation(out=der[:, 4:8], in_=der[:, 4:8], func=AF.Ln, bias=1.0, scale=1.0),
             reads=["der"], writes=["der"])
        P.op("dve", lambda E: E.scalar_tensor_tensor(out=der[:, 0:4], in0=der[:, 0:4], scalar=0.0, in1=der[:, 4:8],
                                                     op0=ALU.max, op1=ALU.add), reads=["der"], writes=["der"])
        P.op("dve", lambda E: E.tensor_scalar(out=der[:, 4:8], in0=der[:, 0:4], scalar1=-4.0, scalar2=None, op0=ALU.mult),
             reads=["der"], writes=["der"])
        P.op("dve", lambda E: E.tensor_scalar(out=der[:, 0:4], in0=der[:, 0:4], scalar1=-8.0, scalar2=None, op0=ALU.mult),
             reads=["der"], writes=["der"])
        P.op("dve", lambda E: E.tensor_scalar(out=der[:, 8:12], in0=par[:, P_BGA:P_BGA + 4], scalar1=0.5, scalar2=None,
                                              op0=ALU.mult), reads=["par", "der"], writes=["der"])
        P.op("dve", lambda E: E.tensor_scalar(out=der[:, 12:16], in0=par[:, P_BGX:P_BGX + 4], scalar1=0.5, scalar2=None,
                                              op0=ALU.mult), reads=["par", "der"], writes=["der"])
        D_CL, D_HCL, D_HBA, D_HBX = 0, 4, 8, 12

        def dc(col):
            return der[:, col:col + 1]

        P.op("act", lambda E: E.activation(out=scb[:, :, :], in_=c_sb[:, :, :], func=AF.Silu), reads=["c"], writes=["scb"])

        ada_reqs = {}

        def ada_request(mv, q):
            c0 = mv * D + q * 256
            ada_reqs[(mv, q)] = ring.request([(lambda a: v_k256(a), src_cols(w_ada, c0, 256))])

        def ada_consume(mv, q):
            h = ada_reqs.pop((mv, q))
            s = h["slot"]
            wv = v_k256(wring[:, s, :])
            for m2 in range(2):
                kc = q * 2 + m2
                b = bank()
                P.mm(psb[b][:, 0:17], [(wv[:, k, m2 * 128:(m2 + 1) * 128], scb[:, k, :]) for k in range(8)],
                     reads=[("W", s), "scb"], writes=[("ps", b)])
                idx = mv * 8 + kc
                P.op("act", lambda E, b=b, idx=idx: E.activation(out=mod[:, idx, :], in_=psb[b][:, 0:17], func=AF.Identity,
                                                                 bias=pc(P_BADA + idx), scale=1.0),
                     reads=[("ps", b), "par"], writes=[("mod", mv)])
                bfree(b)
            ring.release(h)
            if q == 3:
                i = mv // 3
                if mv % 3 == 1:
                    P.op("dve", lambda E, i=i, mv=mv: E.tensor_scalar(
                        out=gs[:, i, :, :], in0=mod[:, mv * 8:(mv + 1) * 8, :], scalar1=1.0, scalar2=None, op0=ALU.add),
                        reads=[("mod", mv)], writes=[("gs", i)])
                    P.op("dve", lambda E, i=i: E.tensor_tensor(
                        out=gs[:, i, :, :], in0=gs[:, i, :, :],
                        in1=par[:, P_G[i]:P_G[i] + 8].unsqueeze(2).broadcast_to([128, 8, 17]), op=ALU.mult),
                        reads=[("gs", i), "par"], writes=[("gs", i)])
                elif mv % 3 == 2:
                    fac = 1.0 if i == 1 else 0.5
                    P.op("dve", lambda E, i=i, mv=mv, fac=fac: E.tensor_scalar(
                        out=gm[:, i, :, :], in0=mod[:, mv * 8:(mv + 1) * 8, :], scalar1=fac, scalar2=None, op0=ALU.mult),
                        reads=[("mod", mv)], writes=[("gm", i)])


        XS4 = x_sb[:, :, TO[4]:TO[4] + 64]
        HS4 = hn_sb[:, :, TO[4]:TO[4] + 64]
        XKEYS4 = [("x", k, 4) for k in range(8)]

        def v8(ap):
            return ap.rearrange("p (k n) -> p k n", k=8)

        def norm_N1(i, t, cx):
            W = TW[t]
            b = bank()
            if t == 4:
                q = bpool.get()
                P.op("act", lambda E: E.activation(out=v8(bt(q)[:, 0:512]), in_=XS4, func=AF.Square),
                     reads=XKEYS4, writes=[bk(q)])
                for k in range(8):
                    P.mm(psb[b][:, 0:W], [(ones_bf[:, :], bt(q)[:, k * 64:(k + 1) * 64])], reads=[bk(q), "ones"],
                         writes=[("ps", b)], start=(k == 0), stop=(k == 7))
                bpool.put(q)
            for k in (range(8) if t != 4 else ()):
                q = bpool.get()
                P.op("act", lambda E, k=k, q=q: E.activation(out=bt(q)[:, 0:W], in_=xs(k, t), func=AF.Square),
                     reads=[("x", k, t)], writes=[bk(q)])
                P.mm(psb[b][:, 0:W], [(ones_bf[:, :], bt(q)[:, 0:W])], reads=[bk(q), "ones"], writes=[("ps", b)],
                     start=(k == 0), stop=(k == 7))
                bpool.put(q)
            r = tpool.get()
            P.op("act", lambda E: E.activation(out=tt(r)[:, 0:W], in_=psb[b][:, 0:W], func=AF.Ln, bias=EPS, scale=1.0 / D),
                 reads=[("ps", b)], writes=[tk(r)])
            bfree(b)
            cx["r"] = r

        def norm_N2(i, t, cx):
            W = TW[t]
            r = cx["r"]
            rb = bank()
            P.op("act", lambda E: E.activation(out=psb[rb][:, 0:W], in_=tt(r)[:, 0:W], func=AF.Exp, scale=-0.5),
                 reads=[tk(r)], writes=[("ps", rb)])
            tpool.put(r)
            cx["rb"] = rb

        def norm_N3(i, t, cx):
            final = (i == 3)
            W = TW[t]
            rb = cx["rb"]
            rs = psb[rb][:, 0:W]
            if t == 4:
                u = tpool.get()
                if final:
                    P.op("dve", lambda E: E.tensor_tensor(out=v8(tt(u)[:, 0:512]), in0=XS4,
                                                          in1=rs.unsqueeze(1).broadcast_to([128, 8, 64]), op=ALU.mult),
                         reads=XKEYS4 + [("ps", rb)], writes=[tk(u)])
                    P.op("dve", lambda E: E.tensor_tensor(out=XS4, in0=v8(tt(u)[:, 0:512]),
                                                          in1=par[:, P_GF:P_GF + 8].unsqueeze(2).broadcast_to([128, 8, 64]),
                                                          op=ALU.mult),
                         reads=[tk(u), "par"], writes=XKEYS4)
                else:
                    Rt = tpool.get()
                    P.op("dve", lambda E: E.tensor_tensor(
                        out=tt(Rt)[:, 0:512].rearrange("p (k s t) -> p k s t", k=8, s=16),
                        in0=gs[:, i, :, 1:17].unsqueeze(3).broadcast_to([128, 8, 16, 4]),
                        in1=v3(rs).unsqueeze(1).broadcast_to([128, 8, 16, 4]), op=ALU.mult),
                        reads=[("gs", i), ("ps", rb)], writes=[tk(Rt)])
                    P.op("dve", lambda E: E.tensor_tensor(out=v8(tt(u)[:, 0:512]), in0=XS4, in1=v8(tt(Rt)[:, 0:512]), op=ALU.mult),
                         reads=XKEYS4 + [tk(Rt)], writes=[tk(u)])
                    P.op("dve", lambda E: E.tensor_tensor(
                        out=HS4.rearrange("p k (s t) -> p k s t", t=4),
                        in0=tt(u)[:, 0:512].rearrange("p (k s t) -> p k s t", k=8, s=16),
                        in1=mod[:, 3 * i * 8:3 * i * 8 + 8, 1:17].unsqueeze(3).broadcast_to([128, 8, 16, 4]), op=ALU.add),
                        reads=[tk(u), ("mod", 3 * i)], writes=[("hn", k, 4) for k in range(8)])
                    tpool.put(Rt)
                tpool.put(u)
                bfree(rb)
                return
            for k in range(8):
                if final:
                    P.op("dve", lambda E, k=k: E.scalar_tensor_tensor(
                        out=xs(k, t), in0=xs(k, t), scalar=pc(P_GF + k), in1=rs, op0=ALU.mult, op1=ALU.mult),
                        reads=[("x", k, t), ("ps", rb), "par"], writes=[("x", k, t)])
                elif t < 4:
                    u = tpool.get()
                    P.op("dve", lambda E, k=k, u=u: E.scalar_tensor_tensor(
                        out=tt(u)[:, 0:W], in0=xs(k, t), scalar=gs[:, i, k, 0:1], in1=rs,
                        op0=ALU.mult, op1=ALU.mult), reads=[("x", k, t), ("ps", rb), ("gs", i)], writes=[tk(u)])
                    P.op("act", lambda E, k=k, u=u: E.activation(
                        out=hs(k, t), in_=tt(u)[:, 0:W], func=AF.Identity, bias=mod[:, 3 * i * 8 + k, 0:1], scale=1.0),
                        reads=[tk(u), ("mod", 3 * i)], writes=[("hn", k, t)])
                    tpool.put(u)
                else:
                    if k == 0:
                        Rt = tpool.get()
                        cx["Rt"] = Rt
                        P.op("dve", lambda E, Rt=Rt: E.tensor_tensor(
                            out=tt(Rt)[:, 0:512].rearrange("p (k s t) -> p k s t", k=8, s=16),
                            in0=gs[:, i, :, 1:17].unsqueeze(3).broadcast_to([128, 8, 16, 4]),
                            in1=v3(rs).unsqueeze(1).broadcast_to([128, 8, 16, 4]), op=ALU.mult),
                            reads=[("gs", i), ("ps", rb)], writes=[tk(Rt)])
                    Rt = cx["Rt"]
                    u = tpool.get()
                    P.op("dve", lambda E, k=k, u=u, Rt=Rt: E.tensor_tensor(out=tt(u)[:, 0:64], in0=xs(k, 4),
                                                                          in1=tt(Rt)[:, k * 64:(k + 1) * 64], op=ALU.mult),
                         reads=[("x", k, 4), tk(Rt)], writes=[tk(u)])
                    P.op("dve", lambda E, k=k, u=u: E.tensor_tensor(out=v3(hs(k, 4)), in0=v3(tt(u)[:, 0:64]),
                                                                   in1=bc(mod[:, 3 * i * 8 + k, 1:17]), op=ALU.add),
                         reads=[tk(u), ("mod", 3 * i)], writes=[("hn", k, 4)])
                    tpool.put(u)
                    if k == 7:
                        tpool.put(Rt)
            bfree(rb)

        def make_norm(i, post=None):
            cxs = {}
            pend = []

            def finish(tp):
                norm_N2(i, tp, cxs[tp])
                norm_N3(i, tp, cxs[tp])
                if post is not None:
                    post(tp)

            def cb(t):
                cxs[t] = {}
                norm_N1(i, t, cxs[t])
                if pend:
                    finish(pend.pop(0))
                pend.append(t)
                if t == NT - 1:
                    finish(pend.pop(0))
            return cb

        def store_y(t):
            W = TW[t]
            for h2 in range(2):
                P.dma("sp", "st_y", yT.rearrange("(k p) n -> p k n", p=128)[:, 4 * h2:4 * h2 + 4, TO[t]:TO[t] + W],
                      x_sb[:, 4 * h2:4 * h2 + 4, TO[t]:TO[t] + W],
                      reads=[("x", k, t) for k in range(4 * h2, 4 * h2 + 4)], writes=[("yT", t, h2)], skip=("st_y",))

        def resid(b, m, t, gi):
            W = TW[t]
            if t < 4:
                P.op("dve", lambda E: E.scalar_tensor_tensor(out=xs(m, t), in0=psb[b][:, 0:W], scalar=gm[:, gi, m, 0:1],
                                                             in1=xs(m, t), op0=ALU.mult, op1=ALU.add),
                     reads=[("ps", b), ("gm", gi), ("x", m, t)], writes=[("x", m, t)])
            else:
                u = tpool.get()
                P.op("dve", lambda E, u=u: E.tensor_tensor(out=v3(tt(u)[:, 0:64]), in0=v3(psb[b][:, 0:64]),
                                                           in1=bc(gm[:, gi, m, 1:17]), op=ALU.mult),
                     reads=[("ps", b), ("gm", gi)], writes=[tk(u)])
                P.op("dve", lambda E, u=u: E.tensor_tensor(out=xs(m, 4), in0=xs(m, 4), in1=tt(u)[:, 0:64], op=ALU.add),
                     reads=[tk(u), ("x", m, 4)], writes=[("x", m, 4)])
                tpool.put(u)

        FT = [(0, 512, (0,)), (512, 512, (1,)), (1024, 512, (2,)), (1536, 288, (3,)), (1824, 288, (3, 4))]

        def resid_ffn(b, m, tf, gi):
            o, W, kt = FT[tf]
            Wp = min(W, 2048 - o)
            tp = kt[0]
            P.op("dve", lambda E: E.scalar_tensor_tensor(out=x_sb[:, m, o:o + Wp], in0=psb[b][:, 0:Wp], scalar=gm[:, gi, m, 0:1],
                                                         in1=x_sb[:, m, o:o + Wp], op0=ALU.mult, op1=ALU.add),
                 reads=[("ps", b), ("gm", gi), ("x", m, tp)], writes=[("x", m, tp)])
            if Wp < W:
                u = tpool.get()
                P.op("dve", lambda E: E.tensor_tensor(out=v3(tt(u)[:, 0:64]), in0=v3(psb[b][:, Wp:Wp + 64]),
                                                      in1=bc(gm[:, gi, m, 1:17]), op=ALU.mult),
                     reads=[("ps", b), ("gm", gi)], writes=[tk(u)])
                P.op("dve", lambda E: E.tensor_tensor(out=xs(m, 4), in0=xs(m, 4), in1=tt(u)[:, 0:64], op=ALU.add),
                     reads=[tk(u), ("x", m, 4)], writes=[("x", m, 4)])
                tpool.put(u)

        def ffn(fi, gi, interleave, after_tile):
            wg, wu, wd = wff[fi]
            NG = 12
            req = {}
            pend_ada = list(interleave) if interleave else []
            for b_ in range(NG + 1):
                if b_ < NG:
                    req[("g", b_)] = ring.request([(lambda a: v_k256(a), src_cols(wg, b_ * 256, 256))])
                    req[("u", b_)] = ring.request([(lambda a: v_k256(a), src_cols(wu, b_ * 256, 256))])
                if b_ >= 1:
                    req[("d", b_ - 1)] = ring.request([(lambda a: v_2x1024(a), src_rows2(wd, (b_ - 1) * 256))])
                for _ in range(4 if b_ == 0 else 2):
                    if pend_ada:
                        ada_request(*pend_ada.pop(0))
            ada_list = list(interleave) if interleave else []

            def gu_step(g, s):
                hg, hu = req[("g", g)], req[("u", g)]
                sg_, su_ = hg["slot"], hu["slot"]
                wgv = v_k256(wring[:, sg_, :])
                wuv = v_k256(wring[:, su_, :])
                f2, tf = s // NT, s % NT
                o, W, kt = FT[tf]
                hn_keys = [("hn", k, t_) for t_ in kt for k in range(8)]
                bg = bank()
                P.mm(psb[bg][:, 0:W], [(wgv[:, k, f2 * 128:(f2 + 1) * 128], hn_sb[:, k, o:o + W]) for k in range(8)],
                     reads=[("W", sg_)] + hn_keys, writes=[("ps", bg)])
                bu = bank()
                P.mm(psb[bu][:, 0:W], [(wuv[:, k, f2 * 128:(f2 + 1) * 128], hn_sb[:, k, o:o + W]) for k in range(8)],
                     reads=[("W", su_)] + hn_keys, writes=[("ps", bu)])
                u = tpool.get()
                P.op("act", lambda E: E.activation(out=tt(u)[:, 0:W], in_=psb[bg][:, 0:W], func=AF.Silu),
                     reads=[("ps", bg)], writes=[tk(u)])
                c = (g % 2) * 2 + f2
                P.op("dve", lambda E: E.tensor_tensor(out=act_sb[:, c, o:o + W], in0=tt(u)[:, 0:W], in1=psb[bu][:, 0:W], op=ALU.mult),
                     reads=[tk(u), ("ps", bu)], writes=[("act", c, t_) for t_ in kt])
                tpool.put(u)
                bfree(bg, bu)
                if s == 2 * NT - 1:
                    ring.release(req.pop(("g", g)))
                    ring.release(req.pop(("u", g)))

            def d_step(gs_, d):
                tf, m = d // 8, d % 8
                o, W, kt = FT[tf]
                b = bank()
                pairs, rd = [], []
                for g in gs_:
                    sd_ = req[("d", g)]["slot"]
                    wdv = v_2x1024(wring[:, sd_, :])
                    par_ = g % 2
                    pairs += [(wdv[:, f2, m * 128:(m + 1) * 128], act_sb[:, par_ * 2 + f2, o:o + W]) for f2 in range(2)]
                    rd += [("W", sd_)] + [("act", par_ * 2 + f2, t_) for f2 in range(2) for t_ in kt]
                P.mm(psb[b][:, 0:W], pairs, reads=rd, writes=[("ps", b)])
                resid_ffn(b, m, tf, gi)
                bfree(b)
                if d == 8 * NT - 1:
                    for g in gs_:
                        ring.release(req.pop(("d", g)))

            for b_ in range(NG + 1):
                for s in range(2 * NT):
                    if b_ < NG:
                        gu_step(b_, s)
                    if b_ >= 1 and b_ != NG - 1:
                        dgs = (NG - 2, NG - 1) if b_ == NG else (b_ - 1,)
                        for d in range(4 * s, 4 * s + 4):
                            d_step(dgs, d)
                            if b_ == NG and d % 8 == 7:
                                t_done = d // 8
                                if t_done >= 1:
                                    after_tile(t_done - 1)
                    if ada_list and (s in (3, 7) or (b_ == 0 and s in (1, 5))):
                        ada_consume(*ada_list.pop(0))
            while ada_list:
                ada_consume(*ada_list.pop(0))
            after_tile(NT - 1)

        def mix_out_tile(hreqs, t):
            W = TW[t]
            for m in range(8):
                b = bank()
                pairs = []
                rd = []
                for c in range(4):
                    h = hreqs[c // 2]
                    wv = v_2x1024(wring[:, h["slot"], :])
                    pairs.append((wv[:, c % 2, m * 128:(m + 1) * 128], acs(c, t)))
                    rd += [("W", h["slot"]), ("act", c, t)]
                P.mm(psb[b][:, 0:W], pairs, reads=rd, writes=[("ps", b)])
                resid(b, m, t, 1)
                bfree(b)

        def lru_A(j, t, h, cx, pcx):
            s = h["slot"]
            wv = v_k256(wring[:, s, :])
            prev_xl = pcx["xl"] if pcx is not None else None
            if True:
                W = TW[t]
                smp = (t == 4)
                b_xl, b_gl = bank(), bank()
                P.mm(psb[b_xl][:, 0:W], [(wv[:, k, 0:128], hs(k, t)) for k in range(8)],
                     reads=[("W", s)] + [("hn", k, t) for k in range(8)], writes=[("ps", b_xl)])
                P.mm(psb[b_gl][:, 0:W], [(wv[:, k, 128:256], hs(k, t)) for k in range(8)],
                     reads=[("W", s)] + [("hn", k, t) for k in range(8)], writes=[("ps", b_gl)])
                xl = tpool.get()
                if not smp:
                    if t == 0:
                        P.op("dve", lambda E, xl=xl: E.memset(tt(xl)[:, 0:3], 0.0), writes=[tk(xl)])
                    else:
                        P.op("act", lambda E, xl=xl, p_=prev_xl: E.activation(out=tt(xl)[:, 0:3], in_=tt(p_)[:, 512:515], func=AF.Copy),
                             reads=[tk(prev_xl)], writes=[tk(xl)])
                    P.op("act", lambda E, xl=xl, b=b_xl: E.activation(out=tt(xl)[:, 3:515], in_=psb[b][:, 0:512], func=AF.Copy),
                         reads=[("ps", b_xl), tk(xl)], writes=[tk(xl)])
                    xl_full = lambda k_, xl=xl: tt(xl)[:, k_:k_ + 512]
                    view = lambda a: a
                    bfree(b_xl)
                else:
                    P.op("act", lambda E, xl=xl: E.activation(out=v3(tt(xl)[:, 0:112], 7)[:, :, 0:3], in_=stc[:, j, :, :], func=AF.Copy),
                         reads=["stc"], writes=[tk(xl)])
                    P.op("act", lambda E, xl=xl, b=b_xl: E.activation(out=v3(tt(xl)[:, 0:112], 7)[:, :, 3:7], in_=v3(psb[b][:, 0:64]),
                                                                       func=AF.Copy),
                         reads=[("ps", b_xl), tk(xl)], writes=[tk(xl)])
                    xl_full = lambda k_, xl=xl: v3(tt(xl)[:, 0:112], 7)[:, :, k_:k_ + 4]
                    view = lambda a: v3(a)
                    bfree(b_xl)
                xc = tpool.get()
                P.op("dve", lambda E, xc=xc, f=xl_full, vw=view, W=W: E.tensor_scalar(
                    out=vw(tt(xc)[:, 0:W]), in0=f(0), scalar1=pc(P_WLC + j * 4 + 0), scalar2=pc(P_BLC + j),
                    op0=ALU.mult, op1=ALU.add), reads=[tk(xl), "par"], writes=[tk(xc)])
                for tap in range(1, 4):
                    P.op("dve", lambda E, xc=xc, f=xl_full, vw=view, W=W, tap=tap: E.scalar_tensor_tensor(
                        out=vw(tt(xc)[:, 0:W]), in0=f(tap), scalar=pc(P_WLC + j * 4 + tap), in1=vw(tt(xc)[:, 0:W]),
                        op0=ALU.mult, op1=ALU.add), reads=[tk(xl), tk(xc), "par"], writes=[tk(xc)])
                if t == 3:
                    so_copy(so[:, j * 3:j * 3 + 3], tt(xl)[:, 512:515], [tk(xl)])
                if smp:
                    so_copy(so[:, 24 + j * 48:24 + (j + 1) * 48].rearrange("p (s k) -> p s k", k=3),
                            v3(tt(xl)[:, 0:112], 7)[:, :, 4:7], [tk(xl)])
                cx.update({"xl": xl, "xc": xc, "b_gl": b_gl, "smp": smp, "W": W})
                if pcx is not None:
                    tpool.put(pcx["xl"])

        def lru_A2(cx):
            xc, W = cx["xc"], cx["W"]
            xb = bpool.get()
            P.op("act", lambda E: E.activation(out=bt(xb)[:, 0:W], in_=tt(xc)[:, 0:W], func=AF.Copy),
                 reads=[tk(xc)], writes=[bk(xb)])
            cx["xb"] = xb

        def lru_B1(j, t, cx):
            xc, xb, b_gl, smp, W = cx["xc"], cx["xb"], cx["b_gl"], cx["smp"], cx["W"]
            if True:
                b_ra, b_ri = bank(), bank()
                P.mm(psb[b_ra][:, 0:W], [(wabd[:, j, :], bt(xb)[:, 0:W])], reads=["wabd", bk(xb)], writes=[("ps", b_ra)])
                P.mm(psb[b_ri][:, 0:W], [(wxbd[:, j, :], bt(xb)[:, 0:W])], reads=["wxbd", bk(xb)], writes=[("ps", b_ri)])
                bpool.put(xb)
                ta, ti, e2 = tpool.get(), tpool.get(), tpool.get()
                P.op("act", lambda E, ta=ta, b=b_ra, W=W: E.activation(out=tt(ta)[:, 0:W], in_=psb[b][:, 0:W], func=AF.Tanh,
                                                                       bias=dc(D_HBA + j), scale=0.5),
                     reads=[("ps", b_ra), "der"], writes=[tk(ta)])
                P.op("act", lambda E, ti=ti, b=b_ri, W=W: E.activation(out=tt(ti)[:, 0:W], in_=psb[b][:, 0:W], func=AF.Tanh,
                                                                       bias=dc(D_HBX + j), scale=0.5),
                     reads=[("ps", b_ri), "der"], writes=[tk(ti)])
                bfree(b_ra, b_ri)
                P.op("act", lambda E, ta=ta, e2=e2, W=W: E.activation(out=tt(e2)[:, 0:W], in_=tt(ta)[:, 0:W], func=AF.Exp,
                                                                      bias=dc(D_CL + j), scale=dc(D_CL + j)),
                     reads=[tk(ta), "der"], writes=[tk(e2)])
                P.op("act", lambda E, ta=ta, W=W: E.activation(out=tt(ta)[:, 0:W], in_=tt(ta)[:, 0:W], func=AF.Exp,
                                                               bias=dc(D_HCL + j), scale=dc(D_HCL + j)),
                     reads=[tk(ta), "der"], writes=[tk(ta)])
                P.op("act", lambda E, e2=e2, W=W: E.activation(out=tt(e2)[:, 0:W], in_=tt(e2)[:, 0:W], func=AF.Relu,
                                                               bias=1.0, scale=-1.0), reads=[tk(e2)], writes=[tk(e2)])
                P.op("act", lambda E, e2=e2, W=W: E.activation(out=tt(e2)[:, 0:W], in_=tt(e2)[:, 0:W], func=AF.Sqrt),
                     reads=[tk(e2)], writes=[tk(e2)])
                gg = tpool.get()
                P.op("act", lambda E, gg=gg, b=b_gl, W=W: E.activation(out=tt(gg)[:, 0:W], in_=psb[b][:, 0:W], func=AF.Gelu_apprx_tanh),
                     reads=[("ps", b_gl)], writes=[tk(gg)])
                bfree(b_gl)
                cx.update({"ta": ta, "ti": ti, "e2": e2, "gg": gg})

        def lru_B2(j, t, cx, pcx):
            prev_h = pcx["hh"] if pcx is not None else None
            xc, smp, W = cx["xc"], cx["smp"], cx["W"]
            ta, ti, e2, gg = cx["ta"], cx["ti"], cx["e2"], cx["gg"]
            if True:
                P.op("dve", lambda E, ti=ti, xc=xc, W=W: E.scalar_tensor_tensor(
                    out=tt(ti)[:, 0:W], in0=tt(ti)[:, 0:W], scalar=1.0, in1=tt(xc)[:, 0:W], op0=ALU.add, op1=ALU.mult),
                    reads=[tk(ti), tk(xc)], writes=[tk(ti)])
                P.op("dve", lambda E, ti=ti, e2=e2, W=W: E.scalar_tensor_tensor(
                    out=tt(ti)[:, 0:W], in0=tt(ti)[:, 0:W], scalar=0.5, in1=tt(e2)[:, 0:W], op0=ALU.mult, op1=ALU.mult),
                    reads=[tk(ti), tk(e2)], writes=[tk(ti)])
                tpool.put(xc)
                hh = tpool.get()
                if not smp:
                    if t == 0:
                        P.op("dve", lambda E, hh=hh, ta=ta, ti=ti: E.tensor_tensor_scan(
                            out=tt(hh)[:, 0:512], data0=tt(ta)[:, 0:512], data1=tt(ti)[:, 0:512], initial=0.0,
                            op0=ALU.mult, op1=ALU.add), reads=[tk(ta), tk(ti)], writes=[tk(hh)])
                    else:
                        P.op("dve", lambda E, hh=hh, ta=ta, ti=ti, ph=prev_h: E.tensor_tensor_scan(
                            out=tt(hh)[:, 0:512], data0=tt(ta)[:, 0:512], data1=tt(ti)[:, 0:512], initial=tt(ph)[:, 511:512],
                            op0=ALU.mult, op1=ALU.add), reads=[tk(ta), tk(ti), tk(prev_h)], writes=[tk(hh)])
                    if t == 3:
                        so_copy(so[:, 12 + j:13 + j], tt(hh)[:, 511:512], [tk(hh)])
                else:
                    a3 = v3(tt(ta)[:, 0:64])
                    b3 = v3(tt(ti)[:, 0:64])
                    P.op("dve", lambda E, hh=hh, a3=a3: E.tensor_tensor(out=tt(hh)[:, 0:16], in0=a3[:, :, 0], in1=sth[:, j, :],
                                                                       op=ALU.mult),
                         reads=[tk(ta), "sth"], writes=[tk(hh)])
                    P.op("dve", lambda E, hh=hh, b3=b3: E.tensor_tensor(out=b3[:, :, 0], in0=b3[:, :, 0], in1=tt(hh)[:, 0:16],
                                                                       op=ALU.add),
                         reads=[tk(hh), tk(ti)], writes=[tk(ti)])
                    P.op("dve", lambda E, a3=a3: E.memset(a3[:, :, 0:1], 0.0), reads=[tk(hh)], writes=[tk(ta)])
                    P.op("dve", lambda E, hh=hh, ta=ta, ti=ti: E.tensor_tensor_scan(
                        out=tt(hh)[:, 0:64], data0=tt(ta)[:, 0:64], data1=tt(ti)[:, 0:64], initial=0.0,
                        op0=ALU.mult, op1=ALU.add), reads=[tk(ta), tk(ti)], writes=[tk(hh)])
                    so_copy(so[:, 216 + j * 16:216 + (j + 1) * 16], v3(tt(hh)[:, 0:64])[:, :, 3], [tk(hh)])
                tpool.put(ta)
                tpool.put(ti)
                tpool.put(e2)
                P.op("dve", lambda E, gg=gg, hh=hh, W=W, t=t: E.tensor_tensor(out=acs(j % 2, t), in0=tt(hh)[:, 0:W], in1=tt(gg)[:, 0:W],
                                                                           op=ALU.mult),
                     reads=[tk(gg), tk(hh)], writes=[("act", j % 2, t)])
                tpool.put(gg)
                cx["hh"] = hh
                if pcx is not None:
                    tpool.put(pcx["hh"])

        def s_pe(j, t, ha, hb, cx):
            sa, sb_ = ha["slot"], hb["slot"]
            wa = v_k256(wring[:, sa, :])
            wb = v_k256(wring[:, sb_, :])
            W = TW[t]
            b_b, b_c, b_v = bank(), bank(), bank()
            hr = [("hn", k, t) for k in range(8)]
            P.mm(psb[b_c][:, 0:W], [(wa[:, k, 128:256], hs(k, t)) for k in range(8)], reads=[("W", sa)] + hr, writes=[("ps", b_c)])
            P.mm(psb[b_v][:, 0:W], [(wb[:, k, 0:128], hs(k, t)) for k in range(8)], reads=[("W", sb_)] + hr, writes=[("ps", b_v)])
            P.mm(psb[b_b][:, 0:W], [(wa[:, k, 0:128], hs(k, t)) for k in range(8)], reads=[("W", sa)] + hr, writes=[("ps", b_b)])
            cx.update({"b_b": b_b, "b_c": b_c, "b_v": b_v})

        def s_rest(j, t, cx, pcx):
            W = TW[t]
            smp = (t == 4)
            b_b, b_c, b_v = cx["b_b"], cx["b_c"], cx["b_v"]
            prev_u = pcx["u"] if pcx is not None else None
            oc = 2 + j % 2
            if True:
                cs = tpool.get()
                P.op("dve", lambda E, cs=cs, b=b_c, W=W: E.tensor_copy(out=tt(cs)[:, 0:W], in_=psb[b][:, 0:W]),
                     reads=[("ps", b_c)], writes=[tk(cs)])
                bfree(b_c)
                u = tpool.get()
                if not smp:
                    if t == 0:
                        P.op("dve", lambda E, u=u: E.memset(tt(u)[:, 0:2], 0.0), writes=[tk(u)])
                    else:
                        P.op("act", lambda E, u=u, p_=prev_u: E.activation(out=tt(u)[:, 0:2], in_=tt(p_)[:, 512:514], func=AF.Copy),
                             reads=[tk(prev_u)], writes=[tk(u)])
                    P.op("dve", lambda E, u=u, cs=cs, b=b_v: E.tensor_tensor(out=tt(u)[:, 2:514], in0=tt(cs)[:, 0:512], in1=psb[b][:, 0:512],
                                                                            op=ALU.mult),
                         reads=[tk(cs), ("ps", b_v), tk(u)], writes=[tk(u)])
                    uf = lambda k_, u=u: tt(u)[:, k_:k_ + 512]
                    view = lambda a: a
                    bfree(b_v)
                else:
                    P.op("act", lambda E, u=u: E.activation(out=v3(tt(u)[:, 0:96], 6)[:, :, 0:2], in_=sts[:, j, :, :], func=AF.Copy),
                         reads=["sts"], writes=[tk(u)])
                    P.op("dve", lambda E, u=u, cs=cs, b=b_v: E.tensor_tensor(out=v3(tt(u)[:, 0:96], 6)[:, :, 2:6], in0=v3(tt(cs)[:, 0:64]),
                                                                            in1=v3(psb[b][:, 0:64]), op=ALU.mult),
                         reads=[tk(cs), ("ps", b_v), tk(u)], writes=[tk(u)])
                    uf = lambda k_, u=u: v3(tt(u)[:, 0:96], 6)[:, :, k_:k_ + 4]
                    view = lambda a: v3(a)
                    bfree(b_v)
                P.op("dve", lambda E, cs=cs, f=uf, vw=view, W=W: E.tensor_scalar(
                    out=vw(tt(cs)[:, 0:W]), in0=f(0), scalar1=pc(P_WSC + j * 3 + 0), scalar2=None, op0=ALU.mult),
                    reads=[tk(u), "par", tk(cs)], writes=[tk(cs)])
                for tap in range(1, 3):
                    P.op("dve", lambda E, cs=cs, f=uf, vw=view, W=W, tap=tap: E.scalar_tensor_tensor(
                        out=vw(tt(cs)[:, 0:W]), in0=f(tap), scalar=pc(P_WSC + j * 3 + tap), in1=vw(tt(cs)[:, 0:W]),
                        op0=ALU.mult, op1=ALU.add), reads=[tk(u), tk(cs), "par"], writes=[tk(cs)])
                P.op("dve", lambda E, cs=cs, b=b_b, W=W, t=t: E.tensor_tensor(out=acs(oc, t), in0=tt(cs)[:, 0:W], in1=psb[b][:, 0:W],
                                                                           op=ALU.mult),
                     reads=[tk(cs), ("ps", b_b)], writes=[("act", oc, t)])
                tpool.put(cs)
                bfree(b_b)
                if t == 3:
                    so_copy(so[:, 16 + j * 2:18 + j * 2], tt(u)[:, 512:514], [tk(u)])
                if smp:
                    so_copy(so[:, 280 + j * 32:280 + (j + 1) * 32].rearrange("p (s k) -> p s k", k=2),
                            v3(tt(u)[:, 0:96], 6)[:, :, 4:6], [tk(u)])
                if prev_u is not None:
                    tpool.put(prev_u)
                cx["u"] = u
                if smp:
                    tpool.put(u)

        def mixer(after_tile):
            lreq, sreq, moq = {}, {}, {}

            def req_chunk(j):
                lreq[j] = ring.request([(lambda a: v_k256(a)[:, :, 0:128], src_cols(wmi, j * 128, 128)),
                                        (lambda a: v_k256(a)[:, :, 128:256], src_cols(wmi, 512 + j * 128, 128))])
                ha = ring.request([(lambda a: v_k256(a)[:, :, 0:128], src_cols(wmi, 1024 + j * 128, 128)),
                                   (lambda a: v_k256(a)[:, :, 128:256], src_cols(wmi, 1536 + j * 128, 128))])
                hb = ring.request([(lambda a: v_k256(a)[:, :, 0:128], src_cols(wmi, 2048 + j * 128, 128))])
                sreq[j] = (ha, hb)

            def req_mo(S):
                moq[S] = [ring.request([(lambda a: v_2x1024(a), src_rows2(wmo, r0))]) for r0 in (S * 256, 512 + S * 256)]

            req_chunk(0)
            req_chunk(1)
            req_mo(0)
            req_chunk(2)
            req_chunk(3)
            req_mo(1)

            units = [(j, t) for j in range(4) for t in range(NT)]
            lcx, scx = {}, {}

            def prev_of(d, u):
                return d.get((u[0], u[1] - 1)) if u[1] > 0 else None

            def do_B2(u):
                j, t = u
                lru_B2(j, t, lcx[u], prev_of(lcx, u))
                if t == NT - 1:
                    tpool.put(lcx[u]["hh"])

            for i, u in enumerate(units):
                j, t = u
                lcx[u] = {}
                scx[u] = {}
                lru_A(j, t, lreq[j], lcx[u], prev_of(lcx, u))
                if t == NT - 1:
                    tpool.put(lcx[u]["xl"])
                    ring.release(lreq[j])
                if i >= 1:
                    pu = units[i - 1]
                    lru_B1(pu[0], pu[1], lcx[pu])
                s_pe(j, t, sreq[j][0], sreq[j][1], scx[u])
                if t == NT - 1:
                    ring.release(sreq[j][0])
                    ring.release(sreq[j][1])
                if (j, t) == (1, NT - 1):
                    mix_out_tile(moq[0], 0)
                elif j == 2 and t + 1 < NT:
                    mix_out_tile(moq[0], t + 1)
                    if t + 1 == NT - 1:
                        for h in moq[0]:
                            ring.release(h)
                elif (j, t) == (3, NT - 1):
                    mix_out_tile(moq[1], 0)
                s_rest(j, t, scx[u], prev_of(scx, u))
                if i >= 1:
                    do_B2(units[i - 1])
                lru_A2(lcx[u])
            lru_B1(units[-1][0], units[-1][1], lcx[units[-1]])
            do_B2(units[-1])
            for t in range(NT):
                if t >= 1:
                    mix_out_tile(moq[1], t)
                    after_tile(t - 1)
            after_tile(NT - 1)
            for h in moq[1]:
                ring.release(h)

        n0 = {t: {} for t in range(NT)}
        ada_order = [(mv, q) for mv in range(9) for q in range(4)]
        for (mv, q) in ada_order[:8]:
            ada_request(mv, q)
        for t in range(NT):
            norm_N1(0, t, n0[t])
            norm_N2(0, t, n0[t])
        for (mv, q) in ada_order[:8]:
            ada_consume(mv, q)
        for t in range(NT):
            norm_N3(0, t, n0[t])
        ada_rest = ada_order[8:]

        ffn(0, 0, ada_rest, make_norm(1))
        mixer(make_norm(2))
        P.dma("sp", "st_so", so_d[:, :], so[:, :], reads=list(so_keys), writes=["so_d"])
        ffn(1, 2, None, make_norm(3, post=store_y))
        P.final_wait("sp", [("yT", t, h2) for t in range(NT) for h2 in range(2)] + ["so_d"])

        @block.sync
        def _(E):
            P.run("sp", E)

        @block.gpsimd
        def _(E):
            P.run("pool", E)

        @block.vector
        def _(E):
            P.run("dve", E)

        @block.scalar
        def _(E):
            P.run("act", E)

        @block.tensor
        def _(E):
            P.run("pe", E)
    return nc


def _fm(v, nch):
    return np.ascontiguousarray(v.reshape(nch, 128).T)


def kernel(x_prompt, x_sample, state_lru_conv, state_lru_h, state_sconv, c_prompt, c_sample,
           w_ada, b_ada, g_ffn1, w1_gate, w1_up, w1_down, g_mix, w_mix_in, w_lru_conv, b_lru_conv,
           w_gate_a, b_gate_a, w_gate_x, b_gate_x, lru_lambda, w_sconv, w_mix_out,
           g_ffn2, w2_gate, w2_up, w2_down, final_gain):
    f32 = np.float32
    A = lambda a: np.asarray(a, dtype=f32)
    x_prompt, x_sample = A(x_prompt), A(x_sample)
    par = np.zeros((128, NPAR), f32)
    par[:, 0:8] = _fm(A(g_ffn1)[0], 8)
    par[:, 8:16] = _fm(A(g_mix)[0], 8)
    par[:, 16:24] = _fm(A(g_ffn2)[0], 8)
    par[:, 24:32] = _fm(A(final_gain), 8)
    par[:, P_BADA:P_BADA + 72] = _fm(A(b_ada)[0], 72)
    wlc = A(w_lru_conv)[0]
    for j in range(4):
        for tap in range(4):
            par[:, P_WLC + j * 4 + tap] = wlc[tap, j * 128:(j + 1) * 128]
    par[:, P_BLC:P_BLC + 4] = _fm(A(b_lru_conv)[0], 4)
    par[:, P_BGA:P_BGA + 4] = _fm(A(b_gate_a)[0].reshape(512), 4)
    par[:, P_BGX:P_BGX + 4] = _fm(A(b_gate_x)[0].reshape(512), 4)
    par[:, P_LAM:P_LAM + 4] = _fm(A(lru_lambda)[0], 4)
    wsc = A(w_sconv)[0]
    for j in range(4):
        for tap in range(3):
            par[:, P_WSC + j * 3 + tap] = wsc[tap, j * 128:(j + 1) * 128]

    def blockdiag(w):
        out = np.zeros((128, 4, 128), f32)
        for j in range(4):
            for hh in range(2):
                out[hh * 64:(hh + 1) * 64, j, hh * 64:(hh + 1) * 64] = w[2 * j + hh]
        return out.reshape(128, 512)

    wabd = blockdiag(A(w_gate_a)[0])
    wxbd = blockdiag(A(w_gate_x)[0])
    shared = {
        "par": par, "wabd": wabd, "wxbd": wxbd,
        "w_ada": np.ascontiguousarray(A(w_ada)[0]),
        "w1g": np.ascontiguousarray(A(w1_gate)[0]), "w1u": np.ascontiguousarray(A(w1_up)[0]),
        "w1d": np.ascontiguousarray(A(w1_down)[0]),
        "w2g": np.ascontiguousarray(A(w2_gate)[0]), "w2u": np.ascontiguousarray(A(w2_up)[0]),
        "w2d": np.ascontiguousarray(A(w2_down)[0]),
        "wmi": np.ascontiguousarray(A(w_mix_in)[0]), "wmo": np.ascontiguousarray(A(w_mix_out)[0]),
    }
    slc, slh, ssc = A(state_lru_conv)[0], A(state_lru_h)[0], A(state_sconv)[0]
    c_prompt, c_sample = A(c_prompt), A(c_sample)
    in_maps = []
    for c in range(NCORES):
        ss = slice(c * 16, (c + 1) * 16)
        xT = np.empty((D, NTOK), f32)
        xT[:, 0:2048] = x_prompt[c].T
        xT[:, 2048:] = x_sample[ss].reshape(64, D).T
        cT = np.empty((D, 17), f32)
        cT[:, 0] = c_prompt[c]
        cT[:, 1:] = c_sample[ss].T
        stc = slc[ss].transpose(2, 0, 1).reshape(4, 128, 16, 3).transpose(1, 0, 2, 3).reshape(128, 192)
        sth = slh[ss].T.reshape(4, 128, 16).transpose(1, 0, 2).reshape(128, 64)
        sts = ssc[ss].transpose(2, 0, 1).reshape(4, 128, 16, 2).transpose(1, 0, 2, 3).reshape(128, 128)
        m = dict(shared)
        m.update({"xT": xT, "cT": cT, "stc": np.ascontiguousarray(stc), "sth": np.ascontiguousarray(sth),
                  "sts": np.ascontiguousarray(sts)})
        in_maps.append(m)

    nc = build_program()
    res = run_bass_kernel_spmd(nc, in_maps, core_ids=list(range(NCORES)))

    y_prompt = np.empty((8, 2048, D), f32)
    y_sample = np.empty((128, 4, D), f32)
    conv_p = np.empty((1, 8, 3, 512), f32)
    h_p = np.empty((1, 8, 512), f32)
    sc_p = np.empty((1, 8, 2, 512), f32)
    conv_s = np.empty((1, 128, 3, 512), f32)
    h_s = np.empty((1, 128, 512), f32)
    sc_s = np.empty((1, 128, 2, 512), f32)
    for c in range(NCORES):
        r = res.results[c]
        yT = np.asarray(r["yT"])
        so = np.asarray(r["so"])
        ss = slice(c * 16, (c + 1) * 16)
        y_prompt[c] = yT[:, 0:2048].T
        y_sample[ss] = yT[:, 2048:].T.reshape(16, 4, D)
        conv_p[0, c] = so[:, 0:12].reshape(128, 4, 3).transpose(2, 1, 0).reshape(3, 512)
        h_p[0, c] = so[:, 12:16].T.reshape(512)
        sc_p[0, c] = so[:, 16:24].reshape(128, 4, 2).transpose(2, 1, 0).reshape(2, 512)
        conv_s[0, ss] = so[:, 24:216].reshape(128, 4, 16, 3).transpose(2, 3, 1, 0).reshape(16, 3, 512)
        h_s[0, ss] = so[:, 216:280].reshape(128, 4, 16).transpose(2, 1, 0).reshape(16, 512)
        sc_s[0, ss] = so[:, 280:408].reshape(128, 4, 16, 2).transpose(2, 3, 1, 0).reshape(16, 2, 512)
    return (y_prompt, y_sample, conv_p, h_p, sc_p, conv_s, h_s, sc_s)
```

```python
import numpy as np
from contextlib import ExitStack
import concourse.bass as bass
import concourse.mybir as mybir
from concourse.bass_utils import run_bass_kernel_spmd

F32 = mybir.dt.float32
BF16 = mybir.dt.bfloat16
AF = mybir.ActivationFunctionType
ALU = mybir.AluOpType

NCORES = 8
D = 1024
DFF = 3072
NTOK = 2112
TW = [512, 512, 512, 512, 64]
TO = [0, 512, 1024, 1536, 2048]
NT = 5
EPS = 1e-6
NSLOT = 8
NTMP = 18
NBT = 6
TMPW = 516

P_G = [0, 8, 16]
P_GF = 24
P_BADA = 32
P_WLC = 104
P_BLC = 120
P_BGA = 124
P_BGX = 128
P_LAM = 132
P_WSC = 136
NPAR = 148
NSO = 408

ENGS = ("pe", "act", "dve", "pool", "sp")


class Prog:
    def __init__(self, nc, stack):
        self.nc = nc
        self.stack = stack
        self.streams = {e: [] for e in ENGS}
        self.sem = {e: stack.enter_context(nc.semaphore("s_" + e)) for e in ENGS}
        self.cnt = {e: 0 for e in ENGS}
        self.waited = {e: {} for e in ENGS}
        self.last_w = {}
        self.readers = {}
        self.dsem = {}
        self.dcnt = {}

    def _handle(self, k):
        return self.sem[k] if k in self.sem else self.dsem[k]

    def _deps(self, reads, writes):
        deps = {}

        def add(tok):
            if tok is not None and deps.get(tok[0], 0) < tok[1]:
                deps[tok[0]] = tok[1]
        for k in reads:
            add(self.last_w.get(k))
        for k in writes:
            add(self.last_w.get(k))
            for t in self.readers.get(k, ()):
                add(t)
        return deps

    def _emit_waits(self, eng, deps, skip=()):
        for k, v in deps.items():
            if k in skip or self.waited[eng].get(k, 0) >= v:
                continue
            self.waited[eng][k] = v
            h = self._handle(k)
            self.streams[eng].append(lambda E, h=h, v=v: E.wait_ge(h, v))

    def _record(self, tok, reads, writes):
        for k in writes:
            self.last_w[k] = tok
            self.readers[k] = []
        for k in reads:
            self.readers.setdefault(k, []).append(tok)

    def op(self, eng, fn, reads=(), writes=()):
        deps = self._deps(reads, writes)
        self._emit_waits(eng, deps)
        self.cnt[eng] += 1
        tok = (eng, self.cnt[eng])
        h = self.sem[eng]
        self.streams[eng].append(lambda E, fn=fn, h=h: fn(E).then_inc(h, 1))
        self._record(tok, reads, writes)
        return tok

    def mm(self, out, pairs, reads=(), writes=(), start=True, stop=True):
        deps = self._deps(reads, writes)
        self._emit_waits("pe", deps, skip=("pe",))
        self.cnt["pe"] += 1
        tok = ("pe", self.cnt["pe"])
        h = self.sem["pe"]
        n = len(pairs)
        for i, (l, r) in enumerate(pairs):
            st_ = bool(start and i == 0)
            sp_ = bool(stop and i == n - 1)
            if i == n - 1:
                self.streams["pe"].append(
                    lambda E, l=l, r=r, st_=st_, sp_=sp_: E.matmul(out, l, r, start=st_, stop=sp_).then_inc(h, 1))
            else:
                self.streams["pe"].append(
                    lambda E, l=l, r=r, st_=st_, sp_=sp_: E.matmul(out, l, r, start=st_, stop=sp_))
        self._record(tok, reads, writes)
        return tok

    def dma(self, q, semname, out, in_, reads=(), writes=(), skip=(), **kw):
        if semname not in self.dsem:
            self.dsem[semname] = self.stack.enter_context(self.nc.semaphore("d_" + semname))
            self.dcnt[semname] = 0
        deps = self._deps(reads, writes)
        self._emit_waits(q, deps, skip=skip)
        self.dcnt[semname] += 16
        tok = (semname, self.dcnt[semname])
        h = self.dsem[semname]
        self.streams[q].append(lambda E, h=h: E.dma_start(out=out, in_=in_, **kw).then_inc(h, 16))
        self._record(tok, reads, writes)
        return tok

    def final_wait(self, eng, keys):
        self._emit_waits(eng, self._deps(keys, ()))

    def run(self, eng, E):
        for f in self.streams[eng]:
            f(E)


class FreeList:
    def __init__(self, items):
        self.free = list(items)

    def get(self):
        return self.free.pop(0)

    def put(self, it):
        self.free.append(it)


class WRing:
    def __init__(self, P, slot_ap):
        self.P = P
        self.slot_ap = slot_ap
        self.busy = [False] * NSLOT
        self.nxt = 0
        self.pending = []

    def request(self, parts):
        h = {"parts": parts, "slot": None}
        self.pending.append(h)
        self.pump()
        return h

    def pump(self):
        while self.pending and not self.busy[self.nxt]:
            h = self.pending.pop(0)
            s = self.nxt
            h["slot"] = s
            self.busy[s] = True
            self.nxt = (s + 1) % NSLOT
            for dst_fn, src in h["parts"]:
                self.P.dma("pool", "W%d" % s, dst_fn(self.slot_ap(s)), src,
                           writes=[("W", s)], skip=("W%d" % s,), max_dma_last_dim=4096)

    def release(self, h):
        self.busy[h["slot"]] = False
        self.pump()


def build_program():
    nc = bass.Bass("TRN2", target_bir_lowering=False)

    def din(name, shape):
        return nc.dram_tensor(name, shape, F32, kind="ExternalInput").ap()

    xT = din("xT", [D, NTOK])
    cT = din("cT", [D, 17])
    par_d = din("par", [128, NPAR])
    stc_d = din("stc", [128, 4 * 16 * 3])
    sth_d = din("sth", [128, 4 * 16])
    sts_d = din("sts", [128, 4 * 16 * 2])
    wabd_d = din("wabd", [128, 4 * 128])
    wxbd_d = din("wxbd", [128, 4 * 128])
    w_ada = din("w_ada", [D, 9 * D])
    wff = [(din("w1g", [D, DFF]), din("w1u", [D, DFF]), din("w1d", [DFF, D])),
           (din("w2g", [D, DFF]), din("w2u", [D, DFF]), din("w2d", [DFF, D]))]
    wmi = din("wmi", [D, 2560])
    wmo = din("wmo", [D, D])
    yT = nc.dram_tensor("yT", [D, NTOK], F32, kind="ExternalOutput").ap()
    so_d = nc.dram_tensor("so", [128, NSO], F32, kind="ExternalOutput").ap()

    with ExitStack() as st:
        P = Prog(nc, st)

        def sb(name, shape, dt):
            return st.enter_context(nc.sbuf_tensor(name, shape, dt))

        x_sb = sb("x_sb", [128, 8, NTOK], F32)
        hn_sb = sb("hn_sb", [128, 8, NTOK], BF16)
        act_sb = sb("act_sb", [128, 4, NTOK], BF16)
        wring = sb("wring", [128, NSLOT, 2048], BF16)
        tmp_sb = sb("tmp_sb", [128, NTMP, TMPW], F32)
        btmp_sb = sb("btmp_sb", [128, NBT, 512], BF16)
        par = sb("par_sb", [128, NPAR], F32)
        der = sb("der_sb", [128, 16], F32)
        c_sb = sb("c_sb", [128, 8, 17], F32)
        scb = sb("scb", [128, 8, 17], BF16)
        mod = sb("mod_sb", [128, 72, 17], F32)
        gs = sb("gs_sb", [128, 3, 8, 17], F32)
        gm = sb("gm_sb", [128, 3, 8, 17], F32)
        stc = sb("stc_sb", [128, 4, 16, 3], F32)
        sth = sb("sth_sb", [128, 4, 16], F32)
        sts = sb("sts_sb", [128, 4, 16, 2], F32)
        so = sb("so_sb", [128, NSO], F32)
        ones_bf = sb("ones_bf", [128, 128], BF16)
        wabd = sb("wabd_sb", [128, 4, 128], BF16)
        wxbd = sb("wxbd_sb", [128, 4, 128], BF16)
        psb = [st.enter_context(nc.psum_tensor("ps%d" % i, [128, 512], F32)) for i in range(8)]
        block = st.enter_context(nc.Block())

        so_keys = []

        def so_copy(out, in_, reads):
            k_ = ("so", len(so_keys))
            so_keys.append(k_)
            P.op("pool", lambda E: E.tensor_copy(out=out, in_=in_), reads=reads, writes=[k_])

        tpool = FreeList(range(NTMP))
        bpool = FreeList(range(NBT))
        ppool = FreeList(range(8))

        def bank():
            return ppool.get()

        def bfree(*bs):
            for b_ in bs:
                ppool.put(b_)

        def tk(i):
            return ("tmp", i)

        def tt(i):
            return tmp_sb[:, i, :]

        def bk(i):
            return ("btmp", i)

        def bt(i):
            return btmp_sb[:, i, :]

        def xs(k, t):
            return x_sb[:, k, TO[t]:TO[t] + TW[t]]

        def hs(k, t):
            return hn_sb[:, k, TO[t]:TO[t] + TW[t]]

        def acs(c, t):
            return act_sb[:, c, TO[t]:TO[t] + TW[t]]

        def v3(ap, n=4):
            return ap.rearrange("p (s t) -> p s t", t=n)

        def bc(ap, n=4):
            return ap.unsqueeze(2).broadcast_to([128, 16, n])

        def pc(col):
            return par[:, col:col + 1]

        ring = WRing(P, lambda s: wring[:, s, :])

        def v_k256(a):
            return a.rearrange("p (k c) -> p k c", k=8)

        def v_2x1024(a):
            return a.rearrange("p (k c) -> p k c", k=2)

        def src_cols(w, c0, n):
            return w.rearrange("(k p) c -> p k c", p=128)[:, :, c0:c0 + n]

        def src_rows2(w, r0):
            return w[r0:r0 + 256, :].rearrange("(k p) c -> p k c", p=128)

        P.dma("sp", "ld_par", par[:, :], par_d[:, :], writes=["par"])
        P.dma("sp", "ld_c", c_sb[:, :, :], cT.rearrange("(k p) s -> p k s", p=128), writes=["c"])
        P.dma("sp", "ld_stc", stc[:, :, :, :], stc_d.rearrange("p (j s k) -> p j s k", j=4, s=16), writes=["stc"])
        P.dma("sp", "ld_sth", sth[:, :, :], sth_d.rearrange("p (j s) -> p j s", j=4), writes=["sth"])
        P.dma("sp", "ld_sts", sts[:, :, :, :], sts_d.rearrange("p (j s k) -> p j s k", j=4, s=16), writes=["sts"],
              skip=("ld_st",))
        P.dma("pool", "ld_wa", wabd[:, :, :], wabd_d.rearrange("p (j m) -> p j m", j=4), writes=["wabd"])
        P.dma("pool", "ld_wx", wxbd[:, :, :], wxbd_d.rearrange("p (j m) -> p j m", j=4), writes=["wxbd"],
              skip=("ld_bd",))
        for t in range(NT):
            for h2 in range(2):
                P.dma("sp", "ld_x%d_%d" % (t, h2), x_sb[:, 4 * h2:4 * h2 + 4, TO[t]:TO[t] + TW[t]],
                      xT.rearrange("(k p) n -> p k n", p=128)[:, 4 * h2:4 * h2 + 4, TO[t]:TO[t] + TW[t]],
                      writes=[("x", k, t) for k in range(4 * h2, 4 * h2 + 4)])
        P.op("dve", lambda E: E.memset(ones_bf[:, :], 1.0), writes=["ones"])

        P.op("dve", lambda E: E.tensor_scalar(out=der[:, 0:4], in0=par[:, P_LAM:P_LAM + 4], scalar1=-1.0, scalar2=None, op0=ALU.mult),
             reads=["par"], writes=["der"])
        P.op("dve", lambda E: E.scalar_tensor_tensor(out=der[:, 4:8], in0=der[:, 0:4], scalar=-1.0, in1=der[:, 0:4],
                                                     op0=ALU.mult, op1=ALU.max), reads=["der"], writes=["der"])
        P.op("act", lambda E: E.activation(out=der[:, 4:8], in_=der[:, 4:8], func=AF.Exp, scale=-1.0),
             reads=["der"], writes=["der"])
        P.op("act", lambda E: E.activation(out=der[:, 4:8], in_=der[:, 4:8], func=AF.Ln, bias=1.0, scale=1.0),
             reads=["der"], writes=["der"])
        P.op("dve", lambda E: E.scalar_tensor_tensor(out=der[:, 0:4], in0=der[:, 0:4], scalar=0.0, in1=der[:, 4:8],
                                                     op0=ALU.max, op1=ALU.add), reads=["der"], writes=["der"])
        P.op("dve", lambda E: E.tensor_scalar(out=der[:, 4:8], in0=der[:, 0:4], scalar1=-4.0, scalar2=None, op0=ALU.mult),
             reads=["der"], writes=["der"])
        P.op("dve", lambda E: E.tensor_scalar(out=der[:, 0:4], in0=der[:, 0:4], scalar1=-8.0, scalar2=None, op0=ALU.mult),
             reads=["der"], writes=["der"])
        P.op("dve", lambda E: E.tensor_scalar(out=der[:, 8:12], in0=par[:, P_BGA:P_BGA + 4], scalar1=0.5, scalar2=None,
                                              op0=ALU.mult), reads=["par", "der"], writes=["der"])
        P.op("dve", lambda E: E.tensor_scalar(out=der[:, 12:16], in0=par[:, P_BGX:P_BGX + 4], scalar1=0.5, scalar2=None,
                                              op0=ALU.mult), reads=["par", "der"], writes=["der"])
        D_CL, D_HCL, D_HBA, D_HBX = 0, 4, 8, 12

        def dc(col):
            return der[:, col:col + 1]

        P.op("act", lambda E: E.activation(out=scb[:, :, :], in_=c_sb[:, :, :], func=AF.Silu), reads=["c"], writes=["scb"])

        ada_reqs = {}

        def ada_request(mv, q):
            c0 = mv * D + q * 256
            ada_reqs[(mv, q)] = ring.request([(lambda a: v_k256(a), src_cols(w_ada, c0, 256))])

        def ada_consume(mv, q):
            h = ada_reqs.pop((mv, q))
            s = h["slot"]
            wv = v_k256(wring[:, s, :])
            for m2 in range(2):
                kc = q * 2 + m2
                b = bank()
                P.mm(psb[b][:, 0:17], [(wv[:, k, m2 * 128:(m2 + 1) * 128], scb[:, k, :]) for k in range(8)],
                     reads=[("W", s), "scb"], writes=[("ps", b)])
                idx = mv * 8 + kc
                P.op("act", lambda E, b=b, idx=idx: E.activation(out=mod[:, idx, :], in_=psb[b][:, 0:17], func=AF.Identity,
                                                                 bias=pc(P_BADA + idx), scale=1.0),
                     reads=[("ps", b), "par"], writes=[("mod", mv)])
                bfree(b)
            ring.release(h)
            if q == 3:
                i = mv // 3
                if mv % 3 == 1:
                    P.op("dve", lambda E, i=i, mv=mv: E.tensor_scalar(
                        out=gs[:, i, :, :], in0=mod[:, mv * 8:(mv + 1) * 8, :], scalar1=1.0, scalar2=None, op0=ALU.add),
                        reads=[("mod", mv)], writes=[("gs", i)])
                    P.op("dve", lambda E, i=i: E.tensor_tensor(
                        out=gs[:, i, :, :], in0=gs[:, i, :, :],
                        in1=par[:, P_G[i]:P_G[i] + 8].unsqueeze(2).broadcast_to([128, 8, 17]), op=ALU.mult),
                        reads=[("gs", i), "par"], writes=[("gs", i)])
                elif mv % 3 == 2:
                    fac = 1.0 if i == 1 else 0.5
                    P.op("dve", lambda E, i=i, mv=mv, fac=fac: E.tensor_scalar(
                        out=gm[:, i, :, :], in0=mod[:, mv * 8:(mv + 1) * 8, :], scalar1=fac, scalar2=None, op0=ALU.mult),
                        reads=[("mod", mv)], writes=[("gm", i)])


        XS4 = x_sb[:, :, TO[4]:TO[4] + 64]
        HS4 = hn_sb[:, :, TO[4]:TO[4] + 64]
        XKEYS4 = [("x", k, 4) for k in range(8)]

        def v8(ap):
            return ap.rearrange("p (k n) -> p k n", k=8)

        def norm_N1(i, t, cx):
            W = TW[t]
            b = bank()
            if t == 4:
                q = bpool.get()
                P.op("act", lambda E: E.activation(out=v8(bt(q)[:, 0:512]), in_=XS4, func=AF.Square),
                     reads=XKEYS4, writes=[bk(q)])
                for k in range(8):
                    P.mm(psb[b][:, 0:W], [(ones_bf[:, :], bt(q)[:, k * 64:(k + 1) * 64])], reads=[bk(q), "ones"],
                         writes=[("ps", b)], start=(k == 0), stop=(k == 7))
                bpool.put(q)
            for k in (range(8) if t != 4 else ()):
                q = bpool.get()
                P.op("act", lambda E, k=k, q=q: E.activation(out=bt(q)[:, 0:W], in_=xs(k, t), func=AF.Square),
                     reads=[("x", k, t)], writes=[bk(q)])
                P.mm(psb[b][:, 0:W], [(ones_bf[:, :], bt(q)[:, 0:W])], reads=[bk(q), "ones"], writes=[("ps", b)],
                     start=(k == 0), stop=(k == 7))
                bpool.put(q)
            r = tpool.get()
            P.op("act", lambda E: E.activation(out=tt(r)[:, 0:W], in_=psb[b][:, 0:W], func=AF.Ln, bias=EPS, scale=1.0 / D),
                 reads=[("ps", b)], writes=[tk(r)])
            bfree(b)
            cx["r"] = r

        def norm_N2(i, t, cx):
            W = TW[t]
            r = cx["r"]
            rb = bank()
            P.op("act", lambda E: E.activation(out=psb[rb][:, 0:W], in_=tt(r)[:, 0:W], func=AF.Exp, scale=-0.5),
                 reads=[tk(r)], writes=[("ps", rb)])
            tpool.put(r)
            cx["rb"] = rb

        def norm_N3(i, t, cx):
            final = (i == 3)
            W = TW[t]
            rb = cx["rb"]
            rs = psb[rb][:, 0:W]
            if t == 4:
                u = tpool.get()
                if final:
                    P.op("dve", lambda E: E.tensor_tensor(out=v8(tt(u)[:, 0:512]), in0=XS4,
                                                          in1=rs.unsqueeze(1).broadcast_to([128, 8, 64]), op=ALU.mult),
                         reads=XKEYS4 + [("ps", rb)], writes=[tk(u)])
                    P.op("dve", lambda E: E.tensor_tensor(out=XS4, in0=v8(tt(u)[:, 0:512]),
                                                          in1=par[:, P_GF:P_GF + 8].unsqueeze(2).broadcast_to([128, 8, 64]),
                                                          op=ALU.mult),
                         reads=[tk(u), "par"], writes=XKEYS4)
                else:
                    Rt = tpool.get()
                    P.op("dve", lambda E: E.tensor_tensor(
                        out=tt(Rt)[:, 0:512].rearrange("p (k s t) -> p k s t", k=8, s=16),
                        in0=gs[:, i, :, 1:17].unsqueeze(3).broadcast_to([128, 8, 16, 4]),
                        in1=v3(rs).unsqueeze(1).broadcast_to([128, 8, 16, 4]), op=ALU.mult),
                        reads=[("gs", i), ("ps", rb)], writes=[tk(Rt)])
                    P.op("dve", lambda E: E.tensor_tensor(out=v8(tt(u)[:, 0:512]), in0=XS4, in1=v8(tt(Rt)[:, 0:512]), op=ALU.mult),
                         reads=XKEYS4 + [tk(Rt)], writes=[tk(u)])
                    P.op("dve", lambda E: E.tensor_tensor(
                        out=HS4.rearrange("p k (s t) -> p k s t", t=4),
                        in0=tt(u)[:, 0:512].rearrange("p (k s t) -> p k s t", k=8, s=16),
                        in1=mod[:, 3 * i * 8:3 * i * 8 + 8, 1:17].unsqueeze(3).broadcast_to([128, 8, 16, 4]), op=ALU.add),
                        reads=[tk(u), ("mod", 3 * i)], writes=[("hn", k, 4) for k in range(8)])
                    tpool.put(Rt)
                tpool.put(u)
                bfree(rb)
                return
            for k in range(8):
                if final:
                    P.op("dve", lambda E, k=k: E.scalar_tensor_tensor(
                        out=xs(k, t), in0=xs(k, t), scalar=pc(P_GF + k), in1=rs, op0=ALU.mult, op1=ALU.mult),
                        reads=[("x", k, t), ("ps", rb), "par"], writes=[("x", k, t)])
                elif t < 4:
                    u = tpool.get()
                    P.op("dve", lambda E, k=k, u=u: E.scalar_tensor_tensor(
                        out=tt(u)[:, 0:W], in0=xs(k, t), scalar=gs[:, i, k, 0:1], in1=rs,
                        op0=ALU.mult, op1=ALU.mult), reads=[("x", k, t), ("ps", rb), ("gs", i)], writes=[tk(u)])
                    P.op("act", lambda E, k=k, u=u: E.activation(
                        out=hs(k, t), in_=tt(u)[:, 0:W], func=AF.Identity, bias=mod[:, 3 * i * 8 + k, 0:1], scale=1.0),
                        reads=[tk(u), ("mod", 3 * i)], writes=[("hn", k, t)])
                    tpool.put(u)
                else:
                    if k == 0:
                        Rt = tpool.get()
                        cx["Rt"] = Rt
                        P.op("dve", lambda E, Rt=Rt: E.tensor_tensor(
                            out=tt(Rt)[:, 0:512].rearrange("p (k s t) -> p k s t", k=8, s=16),
                            in0=gs[:, i, :, 1:17].unsqueeze(3).broadcast_to([128, 8, 16, 4]),
                            in1=v3(rs).unsqueeze(1).broadcast_to([128, 8, 16, 4]), op=ALU.mult),
                            reads=[("gs", i), ("ps", rb)], writes=[tk(Rt)])
                    Rt = cx["Rt"]
                    u = tpool.get()
                    P.op("dve", lambda E, k=k, u=u, Rt=Rt: E.tensor_tensor(out=tt(u)[:, 0:64], in0=xs(k, 4),
                                                                          in1=tt(Rt)[:, k * 64:(k + 1) * 64], op=ALU.mult),
                         reads=[("x", k, 4), tk(Rt)], writes=[tk(u)])
                    P.op("dve", lambda E, k=k, u=u: E.tensor_tensor(out=v3(hs(k, 4)), in0=v3(tt(u)[:, 0:64]),
                                                                   in1=bc(mod[:, 3 * i * 8 + k, 1:17]), op=ALU.add),
                         reads=[tk(u), ("mod", 3 * i)], writes=[("hn", k, 4)])
                    tpool.put(u)
                    if k == 7:
                        tpool.put(Rt)
            bfree(rb)

        def make_norm(i, post=None):
            cxs = {}
            pend = []

            def finish(tp):
                norm_N2(i, tp, cxs[tp])
                norm_N3(i, tp, cxs[tp])
                if post is not None:
                    post(tp)

            def cb(t):
                cxs[t] = {}
                norm_N1(i, t, cxs[t])
                if pend:
                    finish(pend.pop(0))
                pend.append(t)
                if t == NT - 1:
                    finish(pend.pop(0))
            return cb

        def store_y(t):
            W = TW[t]
            for h2 in range(2):
                P.dma("sp", "st_y", yT.rearrange("(k p) n -> p k n", p=128)[:, 4 * h2:4 * h2 + 4, TO[t]:TO[t] + W],
                      x_sb[:, 4 * h2:4 * h2 + 4, TO[t]:TO[t] + W],
                      reads=[("x", k, t) for k in range(4 * h2, 4 * h2 + 4)], writes=[("yT", t, h2)], skip=("st_y",))

        def resid(b, m, t, gi):
            W = TW[t]
            if t < 4:
                P.op("dve", lambda E: E.scalar_tensor_tensor(out=xs(m, t), in0=psb[b][:, 0:W], scalar=gm[:, gi, m, 0:1],
                                                             in1=xs(m, t), op0=ALU.mult, op1=ALU.add),
                     reads=[("ps", b), ("gm", gi), ("x", m, t)], writes=[("x", m, t)])
            else:
                u = tpool.get()
                P.op("dve", lambda E, u=u: E.tensor_tensor(out=v3(tt(u)[:, 0:64]), in0=v3(psb[b][:, 0:64]),
                                                           in1=bc(gm[:, gi, m, 1:17]), op=ALU.mult),
                     reads=[("ps", b), ("gm", gi)], writes=[tk(u)])
                P.op("dve", lambda E, u=u: E.tensor_tensor(out=xs(m, 4), in0=xs(m, 4), in1=tt(u)[:, 0:64], op=ALU.add),
                     reads=[tk(u), ("x", m, 4)], writes=[("x", m, 4)])
                tpool.put(u)

        FT = [(0, 512, (0,)), (512, 512, (1,)), (1024, 512, (2,)), (1536, 288, (3,)), (1824, 288, (3, 4))]

        def resid_ffn(b, m, tf, gi):
            o, W, kt = FT[tf]
            Wp = min(W, 2048 - o)
            tp = kt[0]
            P.op("dve", lambda E: E.scalar_tensor_tensor(out=x_sb[:, m, o:o + Wp], in0=psb[b][:, 0:Wp], scalar=gm[:, gi, m, 0:1],
                                                         in1=x_sb[:, m, o:o + Wp], op0=ALU.mult, op1=ALU.add),
                 reads=[("ps", b), ("gm", gi), ("x", m, tp)], writes=[("x", m, tp)])
            if Wp < W:
                u = tpool.get()
                P.op("dve", lambda E: E.tensor_tensor(out=v3(tt(u)[:, 0:64]), in0=v3(psb[b][:, Wp:Wp + 64]),
                                                      in1=bc(gm[:, gi, m, 1:17]), op=ALU.mult),
                     reads=[("ps", b), ("gm", gi)], writes=[tk(u)])
                P.op("dve", lambda E: E.tensor_tensor(out=xs(m, 4), in0=xs(m, 4), in1=tt(u)[:, 0:64], op=ALU.add),
                     reads=[tk(u), ("x", m, 4)], writes=[("x", m, 4)])
                tpool.put(u)

        def ffn(fi, gi, interleave, after_tile):
            wg, wu, wd = wff[fi]
            NG = 12
            req = {}
            pend_ada = list(interleave) if interleave else []
            for b_ in range(NG + 1):
                if b_ < NG:
                    req[("g", b_)] = ring.request([(lambda a: v_k256(a), src_cols(wg, b_ * 256, 256))])
                    req[("u", b_)] = ring.request([(lambda a: v_k256(a), src_cols(wu, b_ * 256, 256))])
                if b_ >= 1:
                    req[("d", b_ - 1)] = ring.request([(lambda a: v_2x1024(a), src_rows2(wd, (b_ - 1) * 256))])
                for _ in range(4 if b_ == 0 else 2):
                    if pend_ada:
                        ada_request(*pend_ada.pop(0))
            ada_list = list(interleave) if interleave else []

            def gu_step(g, s):
                hg, hu = req[("g", g)], req[("u", g)]
                sg_, su_ = hg["slot"], hu["slot"]
                wgv = v_k256(wring[:, sg_, :])
                wuv = v_k256(wring[:, su_, :])
                f2, tf = s // NT, s % NT
                o, W, kt = FT[tf]
                hn_keys = [("hn", k, t_) for t_ in kt for k in range(8)]
                bg = bank()
                P.mm(psb[bg][:, 0:W], [(wgv[:, k, f2 * 128:(f2 + 1) * 128], hn_sb[:, k, o:o + W]) for k in range(8)],
                     reads=[("W", sg_)] + hn_keys, writes=[("ps", bg)])
                bu = bank()
                P.mm(psb[bu][:, 0:W], [(wuv[:, k, f2 * 128:(f2 + 1) * 128], hn_sb[:, k, o:o + W]) for k in range(8)],
                     reads=[("W", su_)] + hn_keys, writes=[("ps", bu)])
                u = tpool.get()
                P.op("act", lambda E: E.activation(out=tt(u)[:, 0:W], in_=psb[bg][:, 0:W], func=AF.Silu),
                     reads=[("ps", bg)], writes=[tk(u)])
                c = (g % 2) * 2 + f2
                P.op("dve", lambda E: E.tensor_tensor(out=act_sb[:, c, o:o + W], in0=tt(u)[:, 0:W], in1=psb[bu][:, 0:W], op=ALU.mult),
                     reads=[tk(u), ("ps", bu)], writes=[("act", c, t_) for t_ in kt])
                tpool.put(u)
                bfree(bg, bu)
                if s == 2 * NT - 1:
                    ring.release(req.pop(("g", g)))
                    ring.release(req.pop(("u", g)))

            def d_step(gs_, d):
                tf, m = d // 8, d % 8
                o, W, kt = FT[tf]
                b = bank()
                pairs, rd = [], []
                for g in gs_:
                    sd_ = req[("d", g)]["slot"]
                    wdv = v_2x1024(wring[:, sd_, :])
                    par_ = g % 2
                    pairs += [(wdv[:, f2, m * 128:(m + 1) * 128], act_sb[:, par_ * 2 + f2, o:o + W]) for f2 in range(2)]
                    rd += [("W", sd_)] + [("act", par_ * 2 + f2, t_) for f2 in range(2) for t_ in kt]
                P.mm(psb[b][:, 0:W], pairs, reads=rd, writes=[("ps", b)])
                resid_ffn(b, m, tf, gi)
                bfree(b)
                if d == 8 * NT - 1:
                    for g in gs_:
                        ring.release(req.pop(("d", g)))

            for b_ in range(NG + 1):
                for s in range(2 * NT):
                    if b_ < NG:
                        gu_step(b_, s)
                    if b_ >= 1 and b_ != NG - 1:
                        dgs = (NG - 2, NG - 1) if b_ == NG else (b_ - 1,)
                        for d in range(4 * s, 4 * s + 4):
                            d_step(dgs, d)
                            if b_ == NG and d % 8 == 7:
                                t_done = d // 8
                                if t_done >= 1:
                                    after_tile(t_done - 1)
                    if ada_list and (s in (3, 7) or (b_ == 0 and s in (1, 5))):
                        ada_consume(*ada_list.pop(0))
            while ada_list:
                ada_consume(*ada_list.pop(0))
            after_tile(NT - 1)

        def mix_out_tile(hreqs, t):
            W = TW[t]
            for m in range(8):
                b = bank()
                pairs = []
                rd = []
                for c in range(4):
                    h = hreqs[c // 2]
                    wv = v_2x1024(wring[:, h["slot"], :])
                    pairs.append((wv[:, c % 2, m * 128:(m + 1) * 128], acs(c, t)))
                    rd += [("W", h["slot"]), ("act", c, t)]
                P.mm(psb[b][:, 0:W], pairs, reads=rd, writes=[("ps", b)])
                resid(b, m, t, 1)
                bfree(b)

        def lru_A(j, t, h, cx, pcx):
            s = h["slot"]
            wv = v_k256(wring[:, s, :])
            prev_xl = pcx["xl"] if pcx is not None else None
            if True:
                W = TW[t]
                smp = (t == 4)
                b_xl, b_gl = bank(), bank()
                P.mm(psb[b_xl][:, 0:W], [(wv[:, k, 0:128], hs(k, t)) for k in range(8)],
                     reads=[("W", s)] + [("hn", k, t) for k in range(8)], writes=[("ps", b_xl)])
                P.mm(psb[b_gl][:, 0:W], [(wv[:, k, 128:256], hs(k, t)) for k in range(8)],
                     reads=[("W", s)] + [("hn", k, t) for k in range(8)], writes=[("ps", b_gl)])
                xl = tpool.get()
                if not smp:
                    if t == 0:
                        P.op("dve", lambda E, xl=xl: E.memset(tt(xl)[:, 0:3], 0.0), writes=[tk(xl)])
                    else:
                        P.op("act", lambda E, xl=xl, p_=prev_xl: E.activation(out=tt(xl)[:, 0:3], in_=tt(p_)[:, 512:515], func=AF.Copy),
                             reads=[tk(prev_xl)], writes=[tk(xl)])
                    P.op("act", lambda E, xl=xl, b=b_xl: E.activation(out=tt(xl)[:, 3:515], in_=psb[b][:, 0:512], func=AF.Copy),
                         reads=[("ps", b_xl), tk(xl)], writes=[tk(xl)])
                    xl_full = lambda k_, xl=xl: tt(xl)[:, k_:k_ + 512]
                    view = lambda a: a
                    bfree(b_xl)
                else:
                    P.op("act", lambda E, xl=xl: E.activation(out=v3(tt(xl)[:, 0:112], 7)[:, :, 0:3], in_=stc[:, j, :, :], func=AF.Copy),
                         reads=["stc"], writes=[tk(xl)])
                    P.op("act", lambda E, xl=xl, b=b_xl: E.activation(out=v3(tt(xl)[:, 0:112], 7)[:, :, 3:7], in_=v3(psb[b][:, 0:64]),
                                                                       func=AF.Copy),
                         reads=[("ps", b_xl), tk(xl)], writes=[tk(xl)])
                    xl_full = lambda k_, xl=xl: v3(tt(xl)[:, 0:112], 7)[:, :, k_:k_ + 4]
                    view = lambda a: v3(a)
                    bfree(b_xl)
                xc = tpool.get()
                P.op("dve", lambda E, xc=xc, f=xl_full, vw=view, W=W: E.tensor_scalar(
                    out=vw(tt(xc)[:, 0:W]), in0=f(0), scalar1=pc(P_WLC + j * 4 + 0), scalar2=pc(P_BLC + j),
                    op0=ALU.mult, op1=ALU.add), reads=[tk(xl), "par"], writes=[tk(xc)])
                for tap in range(1, 4):
                    P.op("dve", lambda E, xc=xc, f=xl_full, vw=view, W=W, tap=tap: E.scalar_tensor_tensor(
                        out=vw(tt(xc)[:, 0:W]), in0=f(tap), scalar=pc(P_WLC + j * 4 + tap), in1=vw(tt(xc)[:, 0:W]),
                        op0=ALU.mult, op1=ALU.add), reads=[tk(xl), tk(xc), "par"], writes=[tk(xc)])
                if t == 3:
                    so_copy(so[:, j * 3:j * 3 + 3], tt(xl)[:, 512:515], [tk(xl)])
                if smp:
                    so_copy(so[:, 24 + j * 48:24 + (j + 1) * 48].rearrange("p (s k) -> p s k", k=3),
                            v3(tt(xl)[:, 0:112], 7)[:, :, 4:7], [tk(xl)])
                cx.update({"xl": xl, "xc": xc, "b_gl": b_gl, "smp": smp, "W": W})
                if pcx is not None:
                    tpool.put(pcx["xl"])

        def lru_A2(cx):
            xc, W = cx["xc"], cx["W"]
            xb = bpool.get()
            P.op("act", lambda E: E.activation(out=bt(xb)[:, 0:W], in_=tt(xc)[:, 0:W], func=AF.Copy),
                 reads=[tk(xc)], writes=[bk(xb)])
            cx["xb"] = xb

        def lru_B1(j, t, cx):
            xc, xb, b_gl, smp, W = cx["xc"], cx["xb"], cx["b_gl"], cx["smp"], cx["W"]
            if True:
                b_ra, b_ri = bank(), bank()
                P.mm(psb[b_ra][:, 0:W], [(wabd[:, j, :], bt(xb)[:, 0:W])], reads=["wabd", bk(xb)], writes=[("ps", b_ra)])
                P.mm(psb[b_ri][:, 0:W], [(wxbd[:, j, :], bt(xb)[:, 0:W])], reads=["wxbd", bk(xb)], writes=[("ps", b_ri)])
                bpool.put(xb)
                ta, ti, e2 = tpool.get(), tpool.get(), tpool.get()
                P.op("act", lambda E, ta=ta, b=b_ra, W=W: E.activation(out=tt(ta)[:, 0:W], in_=psb[b][:, 0:W], func=AF.Tanh,
                                                                       bias=dc(D_HBA + j), scale=0.5),
                     reads=[("ps", b_ra), "der"], writes=[tk(ta)])
                P.op("act", lambda E, ti=ti, b=b_ri, W=W: E.activation(out=tt(ti)[:, 0:W], in_=psb[b][:, 0:W], func=AF.Tanh,
                                                                       bias=dc(D_HBX + j), scale=0.5),
                     reads=[("ps", b_ri), "der"], writes=[tk(ti)])
                bfree(b_ra, b_ri)
                P.op("act", lambda E, ta=ta, e2=e2, W=W: E.activation(out=tt(e2)[:, 0:W], in_=tt(ta)[:, 0:W], func=AF.Exp,
                                                                      bias=dc(D_CL + j), scale=dc(D_CL + j)),
                     reads=[tk(ta), "der"], writes=[tk(e2)])
                P.op("act", lambda E, ta=ta, W=W: E.activation(out=tt(ta)[:, 0:W], in_=tt(ta)[:, 0:W], func=AF.Exp,
                                                               bias=dc(D_HCL + j), scale=dc(D_HCL + j)),
                     reads=[tk(ta), "der"], writes=[tk(ta)])
                P.op("act", lambda E, e2=e2, W=W: E.activation(out=tt(e2)[:, 0:W], in_=tt(e2)[:, 0:W], func=AF.Relu,
                                                               bias=1.0, scale=-1.0), reads=[tk(e2)], writes=[tk(e2)])
                P.op("act", lambda E, e2=e2, W=W: E.activation(out=tt(e2)[:, 0:W], in_=tt(e2)[:, 0:W], func=AF.Sqrt),
                     reads=[tk(e2)], writes=[tk(e2)])
                gg = tpool.get()
                P.op("act", lambda E, gg=gg, b=b_gl, W=W: E.activation(out=tt(gg)[:, 0:W], in_=psb[b][:, 0:W], func=AF.Gelu_apprx_tanh),
                     reads=[("ps", b_gl)], writes=[tk(gg)])
                bfree(b_gl)
                cx.update({"ta": ta, "ti": ti, "e2": e2, "gg": gg})

        def lru_B2(j, t, cx, pcx):
            prev_h = pcx["hh"] if pcx is not None else None
            xc, smp, W = cx["xc"], cx["smp"], cx["W"]
            ta, ti, e2, gg = cx["ta"], cx["ti"], cx["e2"], cx["gg"]
            if True:
                P.op("dve", lambda E, ti=ti, xc=xc, W=W: E.scalar_tensor_tensor(
                    out=tt(ti)[:, 0:W], in0=tt(ti)[:, 0:W], scalar=1.0, in1=tt(xc)[:, 0:W], op0=ALU.add, op1=ALU.mult),
                    reads=[tk(ti), tk(xc)], writes=[tk(ti)])
                P.op("dve", lambda E, ti=ti, e2=e2, W=W: E.scalar_tensor_tensor(
                    out=tt(ti)[:, 0:W], in0=tt(ti)[:, 0:W], scalar=0.5, in1=tt(e2)[:, 0:W], op0=ALU.mult, op1=ALU.mult),
                    reads=[tk(ti), tk(e2)], writes=[tk(ti)])
                tpool.put(xc)
                hh = tpool.get()
                if not smp:
                    if t == 0:
                        P.op("dve", lambda E, hh=hh, ta=ta, ti=ti: E.tensor_tensor_scan(
                            out=tt(hh)[:, 0:512], data0=tt(ta)[:, 0:512], data1=tt(ti)[:, 0:512], initial=0.0,
                            op0=ALU.mult, op1=ALU.add), reads=[tk(ta), tk(ti)], writes=[tk(hh)])
                    else:
                        P.op("dve", lambda E, hh=hh, ta=ta, ti=ti, ph=prev_h: E.tensor_tensor_scan(
                            out=tt(hh)[:, 0:512], data0=tt(ta)[:, 0:512], data1=tt(ti)[:, 0:512], initial=tt(ph)[:, 511:512],
                            op0=ALU.mult, op1=ALU.add), reads=[tk(ta), tk(ti), tk(prev_h)], writes=[tk(hh)])
                    if t == 3:
                        so_copy(so[:, 12 + j:13 + j], tt(hh)[:, 511:512], [tk(hh)])
                else:
                    a3 = v3(tt(ta)[:, 0:64])
                    b3 = v3(tt(ti)[:, 0:64])
                    P.op("dve", lambda E, hh=hh, a3=a3: E.tensor_tensor(out=tt(hh)[:, 0:16], in0=a3[:, :, 0], in1=sth[:, j, :],
                                                                       op=ALU.mult),
                         reads=[tk(ta), "sth"], writes=[tk(hh)])
                    P.op("dve", lambda E, hh=hh, b3=b3: E.tensor_tensor(out=b3[:, :, 0], in0=b3[:, :, 0], in1=tt(hh)[:, 0:16],
                                                                       op=ALU.add),
                         reads=[tk(hh), tk(ti)], writes=[tk(ti)])
                    P.op("dve", lambda E, a3=a3: E.memset(a3[:, :, 0:1], 0.0), reads=[tk(hh)], writes=[tk(ta)])
                    P.op("dve", lambda E, hh=hh, ta=ta, ti=ti: E.tensor_tensor_scan(
                        out=tt(hh)[:, 0:64], data0=tt(ta)[:, 0:64], data1=tt(ti)[:, 0:64], initial=0.0,
                        op0=ALU.mult, op1=ALU.add), reads=[tk(ta), tk(ti)], writes=[tk(hh)])
                    so_copy(so[:, 216 + j * 16:216 + (j + 1) * 16], v3(tt(hh)[:, 0:64])[:, :, 3], [tk(hh)])
                tpool.put(ta)
                tpool.put(ti)
                tpool.put(e2)
                P.op("dve", lambda E, gg=gg, hh=hh, W=W, t=t: E.tensor_tensor(out=acs(j % 2, t), in0=tt(hh)[:, 0:W], in1=tt(gg)[:, 0:W],
                                                                           op=ALU.mult),
                     reads=[tk(gg), tk(hh)], writes=[("act", j % 2, t)])
                tpool.put(gg)
                cx["hh"] = hh
                if pcx is not None:
                    tpool.put(pcx["hh"])

        def s_pe(j, t, ha, hb, cx):
            sa, sb_ = ha["slot"], hb["slot"]
            wa = v_k256(wring[:, sa, :])
            wb = v_k256(wring[:, sb_, :])
            W = TW[t]
            b_b, b_c, b_v = bank(), bank(), bank()
            hr = [("hn", k, t) for k in range(8)]
            P.mm(psb[b_c][:, 0:W], [(wa[:, k, 128:256], hs(k, t)) for k in range(8)], reads=[("W", sa)] + hr, writes=[("ps", b_c)])
            P.mm(psb[b_v][:, 0:W], [(wb[:, k, 0:128], hs(k, t)) for k in range(8)], reads=[("W", sb_)] + hr, writes=[("ps", b_v)])
            P.mm(psb[b_b][:, 0:W], [(wa[:, k, 0:128], hs(k, t)) for k in range(8)], reads=[("W", sa)] + hr, writes=[("ps", b_b)])
            cx.update({"b_b": b_b, "b_c": b_c, "b_v": b_v})

        def s_rest(j, t, cx, pcx):
            W = TW[t]
            smp = (t == 4)
            b_b, b_c, b_v = cx["b_b"], cx["b_c"], cx["b_v"]
            prev_u = pcx["u"] if pcx is not None else None
            oc = 2 + j % 2
            if True:
                cs = tpool.get()
                P.op("dve", lambda E, cs=cs, b=b_c, W=W: E.tensor_copy(out=tt(cs)[:, 0:W], in_=psb[b][:, 0:W]),
                     reads=[("ps", b_c)], writes=[tk(cs)])
                bfree(b_c)
                u = tpool.get()
                if not smp:
                    if t == 0:
                        P.op("dve", lambda E, u=u: E.memset(tt(u)[:, 0:2], 0.0), writes=[tk(u)])
                    else:
                        P.op("act", lambda E, u=u, p_=prev_u: E.activation(out=tt(u)[:, 0:2], in_=tt(p_)[:, 512:514], func=AF.Copy),
                             reads=[tk(prev_u)], writes=[tk(u)])
                    P.op("dve", lambda E, u=u, cs=cs, b=b_v: E.tensor_tensor(out=tt(u)[:, 2:514], in0=tt(cs)[:, 0:512], in1=psb[b][:, 0:512],
                                                                            op=ALU.mult),
                         reads=[tk(cs), ("ps", b_v), tk(u)], writes=[tk(u)])
                    uf = lambda k_, u=u: tt(u)[:, k_:k_ + 512]
                    view = lambda a: a
                    bfree(b_v)
                else:
                    P.op("act", lambda E, u=u: E.activation(out=v3(tt(u)[:, 0:96], 6)[:, :, 0:2], in_=sts[:, j, :, :], func=AF.Copy),
                         reads=["sts"], writes=[tk(u)])
                    P.op("dve", lambda E, u=u, cs=cs, b=b_v: E.tensor_tensor(out=v3(tt(u)[:, 0:96], 6)[:, :, 2:6], in0=v3(tt(cs)[:, 0:64]),
                                                                            in1=v3(psb[b][:, 0:64]), op=ALU.mult),
                         reads=[tk(cs), ("ps", b_v), tk(u)], writes=[tk(u)])
                    uf = lambda k_, u=u: v3(tt(u)[:, 0:96], 6)[:, :, k_:k_ + 4]
                    view = lambda a: v3(a)
                    bfree(b_v)
                P.op("dve", lambda E, cs=cs, f=uf, vw=view, W=W: E.tensor_scalar(
                    out=vw(tt(cs)[:, 0:W]), in0=f(0), scalar1=pc(P_WSC + j * 3 + 0), scalar2=None, op0=ALU.mult),
                    reads=[tk(u), "par", tk(cs)], writes=[tk(cs)])
                for tap in range(1, 3):
                    P.op("dve", lambda E, cs=cs, f=uf, vw=view, W=W, tap=tap: E.scalar_tensor_tensor(
                        out=vw(tt(cs)[:, 0:W]), in0=f(tap), scalar=pc(P_WSC + j * 3 + tap), in1=vw(tt(cs)[:, 0:W]),
                        op0=ALU.mult, op1=ALU.add), reads=[tk(u), tk(cs), "par"], writes=[tk(cs)])
                P.op("dve", lambda E, cs=cs, b=b_b, W=W, t=t: E.tensor_tensor(out=acs(oc, t), in0=tt(cs)[:, 0:W], in1=psb[b][:, 0:W],
                                                                           op=ALU.mult),
                     reads=[tk(cs), ("ps", b_b)], writes=[("act", oc, t)])
                tpool.put(cs)
                bfree(b_b)
                if t == 3:
                    so_copy(so[:, 16 + j * 2:18 + j * 2], tt(u)[:, 512:514], [tk(u)])
                if smp:
                    so_copy(so[:, 280 + j * 32:280 + (j + 1) * 32].rearrange("p (s k) -> p s k", k=2),
                            v3(tt(u)[:, 0:96], 6)[:, :, 4:6], [tk(u)])
                if prev_u is not None:
                    tpool.put(prev_u)
                cx["u"] = u
                if smp:
                    tpool.put(u)

        def mixer(after_tile):
            lreq, sreq, moq = {}, {}, {}

            def req_chunk(j):
                lreq[j] = ring.request([(lambda a: v_k256(a)[:, :, 0:128], src_cols(wmi, j * 128, 128)),
                                        (lambda a: v_k256(a)[:, :, 128:256], src_cols(wmi, 512 + j * 128, 128))])
                ha = ring.request([(lambda a: v_k256(a)[:, :, 0:128], src_cols(wmi, 1024 + j * 128, 128)),
                                   (lambda a: v_k256(a)[:, :, 128:256], src_cols(wmi, 1536 + j * 128, 128))])
                hb = ring.request([(lambda a: v_k256(a)[:, :, 0:128], src_cols(wmi, 2048 + j * 128, 128))])
                sreq[j] = (ha, hb)

            def req_mo(S):
                moq[S] = [ring.request([(lambda a: v_2x1024(a), src_rows2(wmo, r0))]) for r0 in (S * 256, 512 + S * 256)]

            req_chunk(0)
            req_chunk(1)
            req_mo(0)
            req_chunk(2)
            req_chunk(3)
            req_mo(1)

            units = [(j, t) for j in range(4) for t in range(NT)]
            lcx, scx = {}, {}

            def prev_of(d, u):
                return d.get((u[0], u[1] - 1)) if u[1] > 0 else None

            def do_B2(u):
                j, t = u
                lru_B2(j, t, lcx[u], prev_of(lcx, u))
                if t == NT - 1:
                    tpool.put(lcx[u]["hh"])

            for i, u in enumerate(units):
                j, t = u
                lcx[u] = {}
                scx[u] = {}
                lru_A(j, t, lreq[j], lcx[u], prev_of(lcx, u))
                if t == NT - 1:
                    tpool.put(lcx[u]["xl"])
                    ring.release(lreq[j])
                if i >= 1:
                    pu = units[i - 1]
                    lru_B1(pu[0], pu[1], lcx[pu])
                s_pe(j, t, sreq[j][0], sreq[j][1], scx[u])
                if t == NT - 1:
                    ring.release(sreq[j][0])
                    ring.release(sreq[j][1])
                if (j, t) == (1, NT - 1):
                    mix_out_tile(moq[0], 0)
                elif j == 2 and t + 1 < NT:
                    mix_out_tile(moq[0], t + 1)
                    if t + 1 == NT - 1:
                        for h in moq[0]:
                            ring.release(h)
                elif (j, t) == (3, NT - 1):
                    mix_out_tile(moq[1], 0)
                s_rest(j, t, scx[u], prev_of(scx, u))
                if i >= 1:
                    do_B2(units[i - 1])
                lru_A2(lcx[u])
            lru_B1(units[-1][0], units[-1][1], lcx[units[-1]])
            do_B2(units[-1])
            for t in range(NT):
                if t >= 1:
                    mix_out_tile(moq[1], t)
                    after_tile(t - 1)
            after_tile(NT - 1)
            for h in moq[1]:
                ring.release(h)

        n0 = {t: {} for t in range(NT)}
        ada_order = [(mv, q) for mv in range(9) for q in range(4)]
        for (mv, q) in ada_order[:8]:
            ada_request(mv, q)
        for t in range(NT):
            norm_N1(0, t, n0[t])
            norm_N2(0, t, n0[t])
        for (mv, q) in ada_order[:8]:
            ada_consume(mv, q)
        for t in range(NT):
            norm_N3(0, t, n0[t])
        ada_rest = ada_order[8:]

        ffn(0, 0, ada_rest, make_norm(1))
        mixer(make_norm(2))
        P.dma("sp", "st_so", so_d[:, :], so[:, :], reads=list(so_keys), writes=["so_d"])
        ffn(1, 2, None, make_norm(3, post=store_y))
        P.final_wait("sp", [("yT", t, h2) for t in range(NT) for h2 in range(2)] + ["so_d"])

        @block.sync
        def _(E):
            P.run("sp", E)

        @block.gpsimd
        def _(E):
            P.run("pool", E)

        @block.vector
        def _(E):
            P.run("dve", E)

        @block.scalar
        def _(E):
            P.run("act", E)

        @block.tensor
        def _(E):
            P.run("pe", E)
    return nc


def _fm(v, nch):
    return np.ascontiguousarray(v.reshape(nch, 128).T)


def kernel(x_prompt, x_sample, state_lru_conv, state_lru_h, state_sconv, c_prompt, c_sample,
           w_ada, b_ada, g_ffn1, w1_gate, w1_up, w1_down, g_mix, w_mix_in, w_lru_conv, b_lru_conv,
           w_gate_a, b_gate_a, w_gate_x, b_gate_x, lru_lambda, w_sconv, w_mix_out,
           g_ffn2, w2_gate, w2_up, w2_down, final_gain):
    f32 = np.float32
    A = lambda a: np.asarray(a, dtype=f32)
    x_prompt, x_sample = A(x_prompt), A(x_sample)
    par = np.zeros((128, NPAR), f32)
    par[:, 0:8] = _fm(A(g_ffn1)[0], 8)
    par[:, 8:16] = _fm(A(g_mix)[0], 8)
    par[:, 16:24] = _fm(A(g_ffn2)[0], 8)
    par[:, 24:32] = _fm(A(final_gain), 8)
    par[:, P_BADA:P_BADA + 72] = _fm(A(b_ada)[0], 72)
    wlc = A(w_lru_conv)[0]
    for j in range(4):
        for tap in range(4):
            par[:, P_WLC + j * 4 + tap] = wlc[tap, j * 128:(j + 1) * 128]
    par[:, P_BLC:P_BLC + 4] = _fm(A(b_lru_conv)[0], 4)
    par[:, P_BGA:P_BGA + 4] = _fm(A(b_gate_a)[0].reshape(512), 4)
    par[:, P_BGX:P_BGX + 4] = _fm(A(b_gate_x)[0].reshape(512), 4)
    par[:, P_LAM:P_LAM + 4] = _fm(A(lru_lambda)[0], 4)
    wsc = A(w_sconv)[0]
    for j in range(4):
        for tap in range(3):
            par[:, P_WSC + j * 3 + tap] = wsc[tap, j * 128:(j + 1) * 128]

    def blockdiag(w):
        out = np.zeros((128, 4, 128), f32)
        for j in range(4):
            for hh in range(2):
                out[hh * 64:(hh + 1) * 64, j, hh * 64:(hh + 1) * 64] = w[2 * j + hh]
        return out.reshape(128, 512)

    wabd = blockdiag(A(w_gate_a)[0])
    wxbd = blockdiag(A(w_gate_x)[0])
    shared = {
        "par": par, "wabd": wabd, "wxbd": wxbd,
        "w_ada": np.ascontiguousarray(A(w_ada)[0]),
        "w1g": np.ascontiguousarray(A(w1_gate)[0]), "w1u": np.ascontiguousarray(A(w1_up)[0]),
        "w1d": np.ascontiguousarray(A(w1_down)[0]),
        "w2g": np.ascontiguousarray(A(w2_gate)[0]), "w2u": np.ascontiguousarray(A(w2_up)[0]),
        "w2d": np.ascontiguousarray(A(w2_down)[0]),
        "wmi": np.ascontiguousarray(A(w_mix_in)[0]), "wmo": np.ascontiguousarray(A(w_mix_out)[0]),
    }
    slc, slh, ssc = A(state_lru_conv)[0], A(state_lru_h)[0], A(state_sconv)[0]
    c_prompt, c_sample = A(c_prompt), A(c_sample)
    in_maps = []
    for c in range(NCORES):
        ss = slice(c * 16, (c + 1) * 16)
        xT = np.empty((D, NTOK), f32)
        xT[:, 0:2048] = x_prompt[c].T
        xT[:, 2048:] = x_sample[ss].reshape(64, D).T
        cT = np.empty((D, 17), f32)
        cT[:, 0] = c_prompt[c]
        cT[:, 1:] = c_sample[ss].T
        stc = slc[ss].transpose(2, 0, 1).reshape(4, 128, 16, 3).transpose(1, 0, 2, 3).reshape(128, 192)
        sth = slh[ss].T.reshape(4, 128, 16).transpose(1, 0, 2).reshape(128, 64)
        sts = ssc[ss].transpose(2, 0, 1).reshape(4, 128, 16, 2).transpose(1, 0, 2, 3).reshape(128, 128)
        m = dict(shared)
        m.update({"xT": xT, "cT": cT, "stc": np.ascontiguousarray(stc), "sth": np.ascontiguousarray(sth),
                  "sts": np.ascontiguousarray(sts)})
        in_maps.append(m)

    nc = build_program()
    res = run_bass_kernel_spmd(nc, in_maps, core_ids=list(range(NCORES)))

    y_prompt = np.empty((8, 2048, D), f32)
    y_sample = np.empty((128, 4, D), f32)
    conv_p = np.empty((1, 8, 3, 512), f32)
    h_p = np.empty((1, 8, 512), f32)
    sc_p = np.empty((1, 8, 2, 512), f32)
    conv_s = np.empty((1, 128, 3, 512), f32)
    h_s = np.empty((1, 128, 512), f32)
    sc_s = np.empty((1, 128, 2, 512), f32)
    for c in range(NCORES):
        r = res.results[c]
        yT = np.asarray(r["yT"])
        so = np.asarray(r["so"])
        ss = slice(c * 16, (c + 1) * 16)
        y_prompt[c] = yT[:, 0:2048].T
        y_sample[ss] = yT[:, 2048:].T.reshape(16, 4, D)
        conv_p[0, c] = so[:, 0:12].reshape(128, 4, 3).transpose(2, 1, 0).reshape(3, 512)
        h_p[0, c] = so[:, 12:16].T.reshape(512)
        sc_p[0, c] = so[:, 16:24].reshape(128, 4, 2).transpose(2, 1, 0).reshape(2, 512)
        conv_s[0, ss] = so[:, 24:216].reshape(128, 4, 16, 3).transpose(2, 3, 1, 0).reshape(16, 3, 512)
        h_s[0, ss] = so[:, 216:280].reshape(128, 4, 16).transpose(2, 1, 0).reshape(16, 512)
        sc_s[0, ss] = so[:, 280:408].reshape(128, 4, 16, 2).transpose(2, 3, 1, 0).reshape(16, 2, 512)
    return (y_prompt, y_sample, conv_p, h_p, sc_p, conv_s, h_s, sc_s)
```
